# Optimizing a Trainium2 kernel written in Bass

```python
import math
import jax, jax.numpy as jnp
from jax import lax
import numpy as np

D_MODEL = 2048
BATCH = 4
SEQ = 2048
DEPTH = 1
DEC_BATCH = 128
DEC_SEQ = 8
PAST_LEN = 16384
PAGE_SIZE = 128

MIX_WIDTH = D_MODEL
RET_HEADS = 4
RET_DK = 256
RET_DV = 256
RET_WIDTH = RET_HEADS * RET_DV
RET_CHUNK = 128
ROPE_BASE = 10000.0
POOL_WINDOWS = (2, 4, 8, 16)
POOL_GROUPS = len(POOL_WINDOWS)
POOL_WIDTH = MIX_WIDTH - RET_WIDTH
POOL_GC = POOL_WIDTH // POOL_GROUPS
POOL_BUF = max(POOL_WINDOWS) - 1
D_FF = int(math.ceil(8 * D_MODEL / 3 / 256) * 256)
IN_WIDTH = 2 * RET_HEADS * RET_DK + 2 * RET_WIDTH + POOL_WIDTH
EPS = 1e-6

kernel_name = "hybrid_retention_pool_decoder_step"


def rms_norm(x, w):
    xf = x.astype(jnp.float32)
    y = xf * lax.rsqrt(jnp.mean(xf * xf, axis=-1, keepdims=True) + EPS)
    return (y * w.astype(jnp.float32)).astype(x.dtype)


def rope(x, pos):
    d = x.shape[-1]
    inv_freq = 1.0 / (ROPE_BASE ** (jnp.arange(0, d, 2, dtype=jnp.float32) / d))
    ang = pos[:, None] * inv_freq[None, :]
    cos = jnp.cos(ang)[None, :, None, :]
    sin = jnp.sin(ang)[None, :, None, :]
    x1, x2 = x[..., : d // 2], x[..., d // 2:]
    return jnp.concatenate([x1 * cos - x2 * sin, x2 * cos + x1 * sin], axis=-1)


def retention_chunk(S, qkv, log_gamma):
    q, k, v = qkv
    C = q.shape[2]
    idx = jnp.arange(C, dtype=jnp.float32)
    diff = idx[:, None] - idx[None, :]
    lg = log_gamma[:, None, None]
    decay = jnp.where(diff[None] >= 0, jnp.exp(jnp.maximum(diff[None], 0.0) * lg), 0.0)
    scores = jnp.einsum('bhid,bhjd->bhij', q, k) * decay[None]
    o_inner = jnp.einsum('bhij,bhje->bhie', scores, v)
    q_dec = jnp.exp((idx[None, :] + 1.0) * log_gamma[:, None])
    o_cross = jnp.einsum('bhid,bhde->bhie', q, S) * q_dec[None, :, :, None]
    k_dec = jnp.exp((C - 1.0 - idx[None, :]) * log_gamma[:, None])
    S_new = jnp.exp(C * log_gamma)[None, :, None, None] * S + jnp.einsum(
        'bhjd,bhje->bhde', k * k_dec[None, :, :, None], v)
    return S_new, o_inner + o_cross


def retention(q, k, v, S0):
    B, H, L, dk = q.shape
    dv = v.shape[-1]
    log_gamma = jnp.log(1.0 - 2.0 ** (-5.0 - jnp.arange(H, dtype=jnp.float32)))
    C = RET_CHUNK if L % RET_CHUNK == 0 else L
    n = L // C

    def to_chunks(t):
        return t.reshape(B, H, n, C, t.shape[-1]).transpose(2, 0, 1, 3, 4)

    S_fin, o = lax.scan(lambda S, c: retention_chunk(S, c, log_gamma), S0,
                        (to_chunks(q), to_chunks(k), to_chunks(v)))
    o = o.transpose(1, 2, 0, 3, 4).reshape(B, H, L, dv)
    return o, S_fin


def multiscale_pool(u, buf, start):
    B, L, _ = u.shape
    xp = jnp.concatenate([buf.astype(jnp.float32), u.astype(jnp.float32)], axis=1)
    cs = jnp.concatenate([jnp.zeros((B, 1, POOL_WIDTH), jnp.float32),
                          jnp.cumsum(xp, axis=1)], axis=1)
    pos = start + jnp.arange(L, dtype=jnp.float32)
    outs = []
    for g, w in enumerate(POOL_WINDOWS):
        sl = slice(g * POOL_GC, (g + 1) * POOL_GC)
        hi = cs[:, POOL_BUF + 1:POOL_BUF + 1 + L, sl]
        lo = cs[:, POOL_BUF + 1 - w:POOL_BUF + 1 - w + L, sl]
        cnt = jnp.minimum(pos + 1.0, float(w))[None, :, None]
        outs.append((hi - lo) / cnt)
    mean = jnp.concatenate(outs, axis=-1)
    pooled = mean - u.astype(jnp.float32)
    new_buf = xp[:, -POOL_BUF:, :]
    return pooled, new_buf


def hybrid_layer(x, S0, pool_buf, start, norm_mix_pre, norm_mix_post, w_in, ret_norm_w,
                 w_pool, pool_scale, w_out, norm_ffn_pre, norm_ffn_post, w_gate, w_up, w_down):
    B, L, _ = x.shape
    pos = start + jnp.arange(L, dtype=jnp.float32)
    h = rms_norm(x, norm_mix_pre)
    proj = h @ w_in
    o1 = RET_HEADS * RET_DK
    o2 = 2 * o1
    o3 = o2 + RET_WIDTH
    o4 = o3 + RET_WIDTH
    q = proj[..., :o1].astype(jnp.float32).reshape(B, L, RET_HEADS, RET_DK)
    k = proj[..., o1:o2].astype(jnp.float32).reshape(B, L, RET_HEADS, RET_DK)
    v = proj[..., o2:o3].astype(jnp.float32).reshape(B, L, RET_HEADS, RET_DV)
    g = proj[..., o3:o4]
    u = proj[..., o4:]
    q = rope(q, pos)
    k = rope(k, pos) * (RET_DK ** -0.5)
    o, S_new = retention(q.transpose(0, 2, 1, 3), k.transpose(0, 2, 1, 3),
                         v.transpose(0, 2, 1, 3), S0.astype(jnp.float32))
    o = o.transpose(0, 2, 1, 3)
    o = o * lax.rsqrt(jnp.mean(o * o, axis=-1, keepdims=True) + EPS)
    o = o * ret_norm_w.astype(jnp.float32).reshape(RET_HEADS, RET_DV)
    ret_out = (jax.nn.silu(g.astype(jnp.float32)) * o.reshape(B, L, RET_WIDTH)).astype(x.dtype)

    pooled, new_buf = multiscale_pool(u, pool_buf, start)
    pooled = pooled.astype(x.dtype).reshape(B, L, POOL_GROUPS, POOL_GC)
    pool_out = jnp.einsum('blgc,gcd->blgd', pooled, w_pool).reshape(B, L, POOL_WIDTH) * pool_scale

    mix = jnp.concatenate([ret_out, pool_out.astype(x.dtype)], axis=-1) @ w_out
    x = x + rms_norm(mix, norm_mix_post)
    hf = rms_norm(x, norm_ffn_pre)
    ff = (jax.nn.silu(hf @ w_gate) * (hf @ w_up)) @ w_down
    x = x + rms_norm(ff, norm_ffn_post)
    return x, S_new, new_buf


def setup_inputs(seed: int = 0) -> dict:
    key = jax.random.key(seed)
    ks = jax.random.split(key, 20)
    f32 = jnp.float32

    def nrm(k, shape, scale):
        return jax.random.normal(k, shape, f32) * scale

    return {
        "x_prompt": nrm(ks[0], (BATCH, SEQ, D_MODEL), 1.0),
        "x_sample": nrm(ks[1], (DEC_BATCH, DEC_SEQ, D_MODEL), 1.0),
        "state_ret": nrm(ks[2], (DEC_BATCH, RET_HEADS, RET_DK, RET_DV), 0.1),
        "state_pool": nrm(ks[3], (DEC_BATCH, POOL_BUF, POOL_WIDTH), 1.0),
        "norm_mix_pre": 1.0 + nrm(ks[4], (D_MODEL,), 0.02),
        "norm_mix_post": 1.0 + nrm(ks[5], (D_MODEL,), 0.02),
        "w_in": nrm(ks[6], (D_MODEL, IN_WIDTH), D_MODEL ** -0.5),
        "ret_norm_w": 1.0 + nrm(ks[7], (RET_WIDTH,), 0.02),
        "w_pool": nrm(ks[8], (POOL_GROUPS, POOL_GC, POOL_GC), POOL_GC ** -0.5),
        "pool_scale": 1.0 + nrm(ks[9], (POOL_WIDTH,), 0.02),
        "w_out": nrm(ks[10], (MIX_WIDTH, D_MODEL), MIX_WIDTH ** -0.5),
        "norm_ffn_pre": 1.0 + nrm(ks[11], (D_MODEL,), 0.02),
        "norm_ffn_post": 1.0 + nrm(ks[12], (D_MODEL,), 0.02),
        "w_gate": nrm(ks[13], (D_MODEL, D_FF), D_MODEL ** -0.5),
        "w_up": nrm(ks[14], (D_MODEL, D_FF), D_MODEL ** -0.5),
        "w_down": nrm(ks[15], (D_FF, D_MODEL), D_FF ** -0.5),
    }


def reference(x_prompt, x_sample, state_ret, state_pool, norm_mix_pre, norm_mix_post, w_in,
              ret_norm_w, w_pool, pool_scale, w_out, norm_ffn_pre, norm_ffn_post,
              w_gate, w_up, w_down):
    weights = (norm_mix_pre, norm_mix_post, w_in, ret_norm_w, w_pool, pool_scale, w_out,
               norm_ffn_pre, norm_ffn_post, w_gate, w_up, w_down)
    yp = x_prompt
    Sp = jnp.zeros((BATCH, RET_HEADS, RET_DK, RET_DV), jnp.float32)
    bp = jnp.zeros((BATCH, POOL_BUF, POOL_WIDTH), x_prompt.dtype)
    for _ in range(DEPTH):
        yp, Sp, bp = hybrid_layer(yp, Sp, bp, 0.0, *weights)
    ys = x_sample
    Ss, bs = state_ret, state_pool
    for _ in range(DEPTH):
        ys, Ss, bs = hybrid_layer(ys, Ss, bs, float(PAST_LEN), *weights)
    new_ret_prompt = Sp.astype(x_prompt.dtype)
    new_pool_prompt = bp.astype(x_prompt.dtype)
    new_ret_sample = Ss.astype(state_ret.dtype)
    new_pool_sample = bs.astype(state_pool.dtype)
    return (yp, ys, new_ret_prompt, new_pool_prompt, new_ret_sample, new_pool_sample)
```

```python
import contextlib
import numpy as np
import ml_dtypes
import concourse.bass as bass
import concourse.mybir as mybir
from concourse.bass_utils import run_bass_kernel_spmd

F32 = mybir.dt.float32
BF16 = mybir.dt.bfloat16
AF = mybir.ActivationFunctionType
ALU = mybir.AluOpType

D = 2048
DFF = 5632
NF = DFF // 128
EPS = 1e-6
NCORES = 8
TPB = 3
NT = TPB * 128

PE, ACT, DVE, POOL, SP = "pe", "act", "dve", "pool", "sp"
COMPUTE = (PE, ACT, DVE, POOL)
EPOCH = 12000
import os
USE_WCACHE = True
DBG = {k: True for k in os.environ.get("KDBG", "").split(",") if k}


class Res:
    __slots__ = ("name", "writer", "readers", "excl")

    def __init__(self, name, parents=()):
        self.name = name
        self.excl = name.startswith("ps") and name[2:].isdigit()
        self.writer = None
        self.readers = []
        for p in parents:
            if p.writer is not None:
                self.readers.append(p.writer)
            self.readers.extend(p.readers)


class Op:
    __slots__ = ("eng", "fn", "deps", "is_dma", "signal", "ticket", "slot", "slotval")

    def __init__(self, eng, fn, is_dma):
        self.eng = eng
        self.fn = fn
        self.is_dma = is_dma
        self.deps = []
        self.signal = False
        self.ticket = None
        self.slot = None
        self.slotval = None


class Prog:
    def __init__(self, nc):
        self.nc = nc
        self.ops = {e: [] for e in (PE, ACT, DVE, POOL, SP)}
        self.n_slots = {SP: 16, POOL: 6, ACT: 4}

    def op(self, eng, fn, reads=(), writes=(), dma=False):
        o = Op(eng, fn, dma)
        rset = set()
        deps = []
        for r in reads:
            if r.writer is not None:
                deps.append(r.writer)
                rset.add(id(r.writer))
            if r.excl:
                deps.extend(x for x in r.readers if x.eng != eng)
        for w in writes:
            if w.writer is not None:
                deps.append(w.writer)
            deps.extend(w.readers)
        seen = set()
        for d in deps:
            if id(d) in seen or d is o:
                continue
            seen.add(id(d))
            if (not d.is_dma) and (not dma) and d.eng == eng:
                if eng == PE:
                    continue
            o.deps.append(d)
            d.signal = True
        for r in reads:
            r.readers.append(o)
        for w in writes:
            w.writer = o
            w.readers = []
        self.ops[eng].append(o)
        return o

    def emit(self, final_wait_eng=SP):
        nc = self.nc
        with contextlib.ExitStack() as st:
            counts = {}
            for e in COMPUTE:
                t = 0
                for o in self.ops[e]:
                    if (not o.is_dma) and o.signal:
                        t += 1
                        o.ticket = t
                counts[e] = t
            esems = {}
            for e in COMPUTE:
                n_ep = max(1, (counts[e] + EPOCH - 1) // EPOCH)
                esems[e] = [st.enter_context(nc.semaphore(f"sem_{e}_{i}")) for i in range(n_ep)]
            slot_sems, slot_cnt = {}, {}
            for e in (SP, POOL, ACT):
                dmas = [o for o in self.ops[e] if o.is_dma]
                if not dmas:
                    continue
                ns = self.n_slots[e]
                slot_sems[e] = [st.enter_context(nc.semaphore(f"dsem_{e}_{i}")) for i in range(ns)]
                slot_cnt[e] = [0] * ns
                for i, o in enumerate(dmas):
                    s = i % ns
                    slot_cnt[e][s] += 1
                    o.slot = (e, s)
                    o.slotval = 16 * slot_cnt[e][s]
            self.stats = {e: (len(self.ops[e]), counts.get(e, 0)) for e in self.ops}
            block = st.enter_context(nc.Block())
            engs = {PE: block.tensor, ACT: block.scalar, DVE: block.vector, POOL: block.gpsimd, SP: block.sync}

            def make(e):
                def body(eng):
                    waited = {}

                    def wait(key, sem, val):
                        if waited.get(key, 0) >= val:
                            return
                        waited[key] = val
                        eng.wait_ge(sem, val)

                    for o in self.ops[e]:
                        for d in o.deps:
                            if d.is_dma:
                                de, ds = d.slot
                                wait(("d", de, ds), slot_sems[de][ds], d.slotval)
                            else:
                                ep = (d.ticket - 1) // EPOCH
                                wait(("c", d.eng, ep), esems[d.eng][ep], d.ticket - ep * EPOCH)
                        if o.is_dma:
                            de, ds = o.slot
                            if o.slotval > 16:
                                wait(("d", de, ds), slot_sems[de][ds], o.slotval - 16)
                            o.fn(eng).then_inc(slot_sems[de][ds], 16)
                        else:
                            ins = o.fn(eng)
                            if o.signal:
                                ep = (o.ticket - 1) // EPOCH
                                ins.then_inc(esems[e][ep], 1)
                    if e == final_wait_eng:
                        for de in slot_sems:
                            for s, sem in enumerate(slot_sems[de]):
                                if slot_cnt[de][s] > 0:
                                    eng.wait_ge(sem, 16 * slot_cnt[de][s])
                return body

            for e in (PE, ACT, DVE, POOL, SP):
                if self.ops[e] or e == final_wait_eng:
                    engs[e](make(e))


def build_nc(npre=3, nmain=3, maxphase=9):
    nc = bass.Bass("TRN2", target_bir_lowering=False)

    def din(name, shape, dt=F32):
        return nc.dram_tensor(name, list(shape), dt, kind="ExternalInput").ap()

    def dout(name, shape, dt=F32):
        return nc.dram_tensor(name, list(shape), dt, kind="ExternalOutput").ap()

    xm = din("xm", [1152, D]); xp = din("xp", [1024, D])
    sret = din("sret", [16, 4, 256, 256]); spool = din("spool", [16, 15, 1024])
    w_in = din("w_in", [D, 5120]); w_out = din("w_out", [D, D])
    w_gate = din("w_gate", [D, DFF]); w_up = din("w_up", [D, DFF]); w_down = din("w_down", [DFF, D])
    w_pool = din("w_pool", [4, 256, 256])
    nvec = [din(n, [D]) for n in ("nmp", "nmpo", "nfp", "nfpo")]
    rnw = din("rnw", [1024]); pscale = din("pscale", [1024])
    c_ident = din("c_ident", [128, 128], BF16)
    c_cosm = din("c_cosm", [128, 1152]); c_sinm = din("c_sinm", [128, 1152])
    c_cosp = din("c_cosp", [128, 1024]); c_sinp = din("c_sinp", [128, 1024])
    c_mask = din("c_mask", [128, 2, 4, 128])
    c_dec = din("c_dec", [128, 16])
    c_bmask = din("c_bmask", [128, 16, 128], BF16)
    c_rowmask = din("c_rowmask", [128, 16])
    c_poolA = din("c_poolA", [128, 4, 6, 128])

    def dscr(name, shape, dt):
        return nc.dram_tensor(name, list(shape), dt, kind="Internal").ap()
    WC_TOTAL = D * 5120 + D * D + 3 * D * DFF
    wc = dscr("wc", [WC_TOTAL], BF16)
    ym = dout("ym", [1152, D]); nrp = dout("nrp", [4, 256, 256]); npp = dout("npp", [15, 1024])
    nrs = dout("nrs", [16, 4, 256, 256]); nps = dout("nps", [16, 15, 1024])

    gam = [1.0 - 2.0 ** (-5.0 - h) for h in range(4)]

    with contextlib.ExitStack() as st:
        def sb(name, shape, dt):
            return st.enter_context(nc.sbuf_tensor(name, list(shape), dt))

        ident = sb("ident", [128, 128], BF16)
        bc4 = sb("bc4", [128, 4, D], F32)
        S = sb("S", [128, 4, 2, 256], F32)
        psc = sb("psc", [128, 8], F32)
        wpool = sb("wpool", [128, 4, 2, 256], BF16)
        mask = sb("mask", [128, 2, 4, 128], F32)
        dec = sb("dec", [128, 16], F32)
        rowmask = sb("rowmask", [128, 16], F32)
        stat = sb("stat", [128, 64], F32)
        hbuf = sb("hbuf", [128, D], BF16)
        junk = hbuf
        X = sb("X", [128, TPB, D], F32)
        Xf = X[:].rearrange("p t f -> p (t f)")
        NSB = 8
        Sb_t = [Xf[:, i * 768:i * 768 + 512] for i in range(NSB)]
        Sbb_t = [Xf[:, i * 768 + 512:i * 768 + 768].bitcast(BF16) for i in range(NSB)]
        HM = sb("HM", [128, 2, 16, NT], BF16)
        HT = HM[:, 0]
        MT = HM[:, 1]
        FFO = HM[:].rearrange("p a k t -> p (a k t)").bitcast(F32).rearrange("p (t f) -> p t f", t=TPB)
        ACTA = sb("ACTA", [128, NF * NT], BF16)
        actT = ACTA[:].rearrange("p (f t) -> p f t", f=NF)
        A32 = ACTA[:].bitcast(F32)
        mixout = A32[:, 0:TPB * D].rearrange("p (t f) -> p t f", t=TPB)
        WB = sb("WB", [128, 3, 16, 512], BF16)
        o_ = [0]

        def carve(n_f32):
            a = o_[0]
            o_[0] += n_f32
            return A32[:, a:a + n_f32]
        cs_t = carve(2 * NT).rearrange("p (a t) -> p a t", a=2)
        ropetmp = carve(4 * NT).rearrange("p (a t) -> p a t", a=4)
        wn = carve(1024)
        pool_base = o_[0]
        u_t = carve(4 * 512).rearrange("p (s c) -> p s c", s=4)
        hist_t = carve(2 * 512).rearrange("p (q c) -> p q c", q=2)
        poolA_b = [carve(6 * 128).rearrange("p (k t) -> p k t", k=6) for _ in range(2)]
        pooledT_b = [carve(NT).bitcast(BF16).rearrange("p (c t) -> p c t", c=2),
                     A32[:, 2 * NT:3 * NT].bitcast(BF16).rearrange("p (c t) -> p c t", c=2)]
        pool_end = o_[0]
        o_[0] = pool_base
        qpad_t = carve(2048).bitcast(BF16).rearrange("p (a b m) -> p a b m", a=2, b=16)
        kmask_t = carve(2048).bitcast(BF16).rearrange("p (a b d) -> p a b d", a=2, b=16)
        bmask = carve(1024).bitcast(BF16).rearrange("p (b m) -> p b m", b=16)
        o_[0] = max(o_[0], pool_end)
        assert o_[0] <= NF * NT // 2, o_[0]
        qT = sb("qT", [128, 2, NT], BF16); kT = sb("kT", [128, 2, NT], BF16)
        ktok = sb("ktok", [128, TPB, 256], BF16)
        vt = sb("vt", [128, TPB, 256], BF16); vd = sb("vd", [128, TPB, 256], BF16)
        Gp = sb("Gp", [128, TPB, 256], F32)
        oA = sb("oA", [128, 2, 256], F32); oB = sb("oB", [128, 2, 256], F32)
        sTm = sb("sTm", [128, 3, 128], BF16); rr = sb("rr", [128, 3, 256], BF16)
        Sbf = sb("Sbf", [128, 3, 2, 256], BF16)
        uprev = sb("uprev", [128, 1024], F32)
        psum = st.enter_context(nc.psum_tensor("psum", [128, 8, 512], F32))

        P = Prog(nc)
        build_nc.sbuf_left = nc.sbuf_bytes_remaining
        R = {}

        def res(name, parents=()):
            R[name] = Res(name, parents)
            return R[name]

        for n in ["ident", "bc4", "S0", "S1", "S2", "S3", "psc", "wpool", "mask", "dec", "rowmask",
                  "hbuf", "uprev", "Sbf", "qT", "kT", "ktok", "vt", "vd", "Gp"]:
            res(n)
        for i in range(2):
            res(f"oA{i}"); res(f"oB{i}")
        for i in range(3):
            res(f"sTm{i}"); res(f"rr{i}"); res(f"Sbf{i}")

        for i in range(8):
            res(f"ps{i}")
        for i in range(3):
            res(f"wb{i}")
        for i in range(64):
            res(f"stat{i}")
        psi = [0]

        def nextps():
            i = psi[0] % 7
            psi[0] += 1
            return psum[:, i, :], R[f"ps{i}"]
        sti = [0]

        def nextstat():
            i = sti[0] % 64
            sti[0] += 1
            return stat[:, i:i + 1], R[f"stat{i}"]
        wbi = [0]

        def dma(eng, out, in_, reads=(), writes=(), **kw):
            return P.op(eng, lambda e: e.dma_start(out=out, in_=in_, **kw), reads=reads, writes=writes, dma=True)

        dma(SP, ident[:], c_ident[:, :], writes=[R["ident"]])
        for i in range(4):
            dma(SP, bc4[:, i, :], nvec[i].partition_broadcast(128), writes=[R["bc4"]])
        dma(SP, psc[:], pscale.rearrange("(c p) -> p c", p=128), writes=[R["psc"]], allow_slow_non_contiguous=True)
        dma(SP, mask[:], c_mask[:, :, :, :], writes=[R["mask"]])
        dma(SP, dec[:], c_dec[:, :], writes=[R["dec"]])
        dma(SP, rowmask[:], c_rowmask[:, :], writes=[R["rowmask"]])
        dma(POOL, wpool[:], w_pool.rearrange("g (c p) d -> p g c d", p=128), writes=[R["wpool"]])
        for h in range(4):
            P.op(DVE, lambda e, h=h: e.memset(S[:, h], 0.0), writes=[R[f"S{h}"]])
        P.op(DVE, lambda e: e.memset(uprev[:], 0.0), writes=[R["uprev"]])

        WSRC = {"w_in": w_in, "w_out": w_out, "w_gate": w_gate, "w_up": w_up, "w_down": w_down}
        wcache = {}
        wc_off = [0]
        wcount = {}
        cache_on = {"w_in": 0, "w_out": 0, "w_gate": 0, "w_up": 1, "w_down": 1}

        def load_w(pieces):
            s = wbi[0] % 3
            wbi[0] += 1
            r = R[f"wb{s}"]
            for (wname, r0, nr, c0, ncol, coff) in pieces:
                nk = nr // 128
                key = (wname, r0, nr, c0, ncol)
                dst = WB[:, s, 0:nk, coff:coff + ncol]
                wf = WSRC[wname]
                if USE_WCACHE and key in wcache:
                    rc, off = wcache[key]
                    dma(POOL, dst, wc[off:off + nr * ncol].rearrange("(p k n) -> p k n", p=128, k=nk), reads=[rc], writes=[r])
                else:
                    dma(POOL, dst, wf[r0:r0 + nr, c0:c0 + ncol].rearrange("(k p) n -> p k n", p=128), writes=[r])
                    cnt = wcount.get(key, 0)
                    wcount[key] = cnt + 1
                    if USE_WCACHE and cnt == cache_on[wname]:
                        off = wc_off[0]
                        wc_off[0] += nr * ncol
                        assert wc_off[0] <= WC_TOTAL
                        wcache[key] = (Res("wc"), off)
                        dma(SP, wc[off:off + nr * ncol].rearrange("(p k n) -> p k n", p=128, k=nk), dst, reads=[r], writes=[wcache[key][0]])
            return s, r

        def rmsnorm_stats(src_ap, src_res, n):
            ss, r_ss = nextstat()
            P.op(ACT, lambda e: e.activation(out=junk[:, 0:n], in_=src_ap, func=AF.Square, accum_out=ss),
                 reads=[src_res], writes=[R["hbuf"], r_ss])
            sq, r_sq = nextstat()
            P.op(ACT, lambda e: e.activation(out=sq, in_=ss, func=AF.Sqrt, scale=1.0 / n, bias=EPS),
                 reads=[r_ss], writes=[r_sq])
            rs, r_rs = nextstat()
            P.op(DVE, lambda e: e.reciprocal(out=rs, in_=sq), reads=[r_sq], writes=[r_rs])
            return rs, r_rs

        def transpose_to(dst_fn, dst_res, src_tile_ap, src_res, nchunks):
            for c0 in range(0, nchunks, 4):
                n = min(4, nchunks - c0)
                pa, pr = nextps()
                pv = pa.bitcast(BF16)

                def tr(e, c0=c0, n=n, pv=pv):
                    ins = None
                    for j in range(n):
                        ins = e.transpose(out=pv[:, j * 128:(j + 1) * 128],
                                          in_=src_tile_ap[:, (c0 + j) * 128:(c0 + j + 1) * 128], identity=ident[:])
                    return ins
                P.op(PE, tr, reads=[src_res, R["ident"]], writes=[pr])
                P.op(ACT, lambda e, c0=c0, n=n, pv=pv: e.activation(
                    out=dst_fn(c0, n), in_=pv[:, 0:n * 128].rearrange("p (j t) -> p j t", j=n), func=AF.Copy),
                    reads=[pr], writes=[dst_res])


        state = {"arena_parents": [], "hm_parents": [], "x_parents": []}

        def do_block(tiles, prefix):
            ntl = len(tiles)
            nt = ntl * 128
            has_sample = any(k == "m" and i == 8 for k, i in tiles)
            r_x = [res(f"x{t}", state["x_parents"]) for t in range(ntl)]
            r_hTt = [res(f"hT{t}", state["hm_parents"]) for t in range(ntl)]
            r_mT = res("mT", state["hm_parents"])
            ap_par = state["arena_parents"]
            r_cs = res("cs", ap_par); r_rt = res("ropetmp", ap_par); r_wn = res("wn", ap_par)
            r_u = res("u", ap_par); r_hist = res("hist", ap_par)
            r_pA_b = [res(f"poolA{i}", ap_par) for i in range(2)]
            r_pT_b = [res("pooledT0", ap_par), r_rt]
            for tl, (kind, ti) in enumerate(tiles):
                src = (xp if kind == "p" else xm)[ti * 128:(ti + 1) * 128, :]
                dma(SP, X[:, tl, :], src, writes=[r_x[tl]])
            cosd, sind = (c_cosp, c_sinp) if prefix else (c_cosm, c_sinm)
            t0 = tiles[0][1] * 128
            dma(SP, cs_t[:, 0, 0:nt], cosd[:, t0:t0 + nt], writes=[r_cs])
            dma(SP, cs_t[:, 1, 0:nt], sind[:, t0:t0 + nt], writes=[r_cs])
            if not prefix:
                dma(SP, wn, rnw.partition_broadcast(128), writes=[r_wn])
            for tl in range(ntl):
                rs, r_rs = rmsnorm_stats(X[:, tl, :], r_x[tl], D)
                P.op(DVE, lambda e, tl=tl, rs=rs: e.scalar_tensor_tensor(
                    out=hbuf[:], in0=X[:, tl, :], scalar=rs, in1=bc4[:, 0, :], op0=ALU.mult, op1=ALU.mult),
                    reads=[r_x[tl], r_rs, R["bc4"]], writes=[R["hbuf"]])
                transpose_to(lambda c0, n, tl=tl: HT[:, c0:c0 + n, tl * 128:(tl + 1) * 128], r_hTt[tl], hbuf, R["hbuf"], 16)

            if maxphase < 1:
                return
            need_u = (not prefix) or tiles[-1] == ("p", 7)
            if need_u:
                for j in range(2):
                    s, r_w = load_w([("w_in", 0, D, 4096 + 512 * j, 512, 0)])
                    if not prefix:
                        P.op(DVE, lambda e, j=j: e.tensor_copy(out=u_t[:, 0, :], in_=uprev[:, j * 512:(j + 1) * 512]),
                             reads=[R["uprev"]], writes=[r_u])
                    for tl in range(ntl):
                        if prefix and tl != ntl - 1:
                            continue
                        pa, pr = nextps()

                        def mmu(e, tl=tl, s=s, pa=pa):
                            ins = None
                            for k in range(16):
                                ins = e.matmul(pa, lhsT=HT[:, k, tl * 128:(tl + 1) * 128], rhs=WB[:, s, k, :],
                                               start=(k == 0), stop=(k == 15))
                            return ins
                        P.op(PE, mmu, reads=[r_hTt[tl], r_w], writes=[pr])
                        if prefix:
                            P.op(ACT, lambda e, pa=pa, j=j: e.activation(out=uprev[:, j * 512:(j + 1) * 512], in_=pa, func=AF.Copy),
                                 reads=[pr], writes=[R["uprev"]])
                        else:
                            P.op(ACT, lambda e, pa=pa, tl=tl: e.activation(out=u_t[:, tl + 1, :], in_=pa, func=AF.Copy),
                                 reads=[pr], writes=[r_u])
                    if prefix:
                        continue
                    for tl, (kind, ti) in enumerate(tiles):
                        if ti == 7:
                            dma(SP, npp[:, j * 512:(j + 1) * 512], u_t[113:128, tl + 1, :], reads=[r_u])
                        if ti == 8:
                            for b in range(16):
                                dma(SP, nps[b, 7:15, j * 512:(j + 1) * 512], u_t[b * 8:(b + 1) * 8, tl + 1, :], reads=[r_u])
                            dma(SP, hist_t[0:120, 0, :], spool[0:8, :, j * 512:(j + 1) * 512].rearrange("b r c -> (b r) c"), writes=[r_hist])
                            dma(SP, hist_t[0:120, 1, :], spool[8:16, :, j * 512:(j + 1) * 512].rearrange("b r c -> (b r) c"), writes=[r_hist])
                            for q in range(2):
                                for bl in range(8):
                                    dma(SP, nps[8 * q + bl, 0:7, j * 512:(j + 1) * 512], hist_t[bl * 15 + 8:bl * 15 + 15, q, :], reads=[r_hist])
                    for gg in range(2):
                        g = 2 * j + gg
                        poolA_t, r_pA = poolA_b[g % 2], r_pA_b[g % 2]
                        pooledT, r_pT = pooledT_b[g % 2], r_pT_b[g % 2]
                        dma(SP, poolA_t, c_poolA[:, g, :, :], writes=[r_pA])
                        for cc in range(2):
                            c0 = gg * 256 + cc * 128
                            pa, pr = nextps()

                            def mmp(e, pa=pa, c0=c0, poolA_t=poolA_t):
                                ins = None
                                for tl, (kind, ti) in enumerate(tiles):
                                    o = pa[:, tl * 128:(tl + 1) * 128]
                                    if ti == 8:
                                        e.matmul(o, lhsT=hist_t[0:120, 0, c0:c0 + 128], rhs=poolA_t[0:120, 3, :], start=True, stop=False)
                                        e.matmul(o, lhsT=hist_t[0:120, 1, c0:c0 + 128], rhs=poolA_t[0:120, 4, :], start=False, stop=False)
                                        ins = e.matmul(o, lhsT=u_t[:, tl + 1, c0:c0 + 128], rhs=poolA_t[:, 5, :], start=False, stop=True)
                                    else:
                                        e.matmul(o, lhsT=u_t[:, tl, c0:c0 + 128], rhs=poolA_t[:, 2, :], start=True, stop=False)
                                        ins = e.matmul(o, lhsT=u_t[:, tl + 1, c0:c0 + 128], rhs=poolA_t[:, 0 if ti == 0 else 1, :],
                                                       start=False, stop=True)
                                return ins
                            P.op(PE, mmp, reads=[r_u, r_hist, r_pA], writes=[pr])
                            P.op(ACT, lambda e, pa=pa, cc=cc, pooledT=pooledT: e.activation(out=pooledT[:, cc, 0:nt], in_=pa[:, 0:nt], func=AF.Copy),
                                 reads=[pr], writes=[r_pT])
                        for dc in range(2):
                            pa, pr = nextps()

                            def mmo(e, pa=pa, g=g, dc=dc, pooledT=pooledT):
                                e.matmul(pa[:, 0:nt], lhsT=wpool[:, g, 0, dc * 128:(dc + 1) * 128], rhs=pooledT[:, 0, 0:nt], start=True, stop=False)
                                return e.matmul(pa[:, 0:nt], lhsT=wpool[:, g, 1, dc * 128:(dc + 1) * 128], rhs=pooledT[:, 1, 0:nt], start=False, stop=True)
                            P.op(PE, mmo, reads=[r_pT, R["wpool"]], writes=[pr])
                            P.op(ACT, lambda e, pa=pa, g=g, dc=dc: e.activation(
                                out=MT[:, 8 + g * 2 + dc, 0:nt], in_=pa[:, 0:nt], func=AF.Copy, scale=psc[:, g * 2 + dc:g * 2 + dc + 1]),
                                reads=[pr, R["psc"]], writes=[r_mT])
                    last_tl = max(tl for tl, (kind, ti) in enumerate(tiles) if ti != 8)
                    P.op(DVE, lambda e, j=j, last_tl=last_tl: e.tensor_copy(out=uprev[:, j * 512:(j + 1) * 512], in_=u_t[:, last_tl + 1, :]),
                         reads=[r_u], writes=[R["uprev"]])

            if maxphase < 2:
                return
            samp_par = [r_u, r_hist] + r_pA_b + r_pT_b
            r_bm = res("bmask", samp_par)
            r_sb = [res(f"sb{i}", r_x) for i in range(NSB)] if has_sample else []
            r_sbb = [res(f"sbb{i}", r_x) for i in range(NSB)] if has_sample else []
            samp = {"next": 0}
            PF = 3

            def samp_load(n):
                hh, bb = divmod(n, 16)
                i = n % NSB
                dma(SP, Sb_t[i].rearrange("p (c e) -> p c e", c=2), sret[bb, hh].rearrange("(c p) e -> p c e", p=128), writes=[r_sb[i]])
            if has_sample:
                dma(SP, bmask, c_bmask[:, :, :], writes=[r_bm])
            r_qp = res("qpad", samp_par); r_km = res("kmask", samp_par)
            sbi = [0]
            deferred = []
            for h in range(4):
                rS = R[f"S{h}"]
                srcs = [("w_in", 0, D, 1024 + h * 256, 256, 256)]
                if not prefix:
                    srcs = [("w_in", 0, D, h * 256, 256, 0)] + srcs
                sA, r_wA = load_w(srcs)
                srcs = [("w_in", 0, D, 2048 + h * 256, 256, 0)]
                if not prefix:
                    srcs.append(("w_in", 0, D, 3072 + h * 256, 256, 256))
                sB, r_wB = load_w(srcs)
                for qk in ([1] if prefix else [0, 1]):
                    dstT, r_dst = (qT, R["qT"]) if qk == 0 else (kT, R["kT"])
                    pas = []
                    for half in range(2):
                        pa, pr = nextps()
                        pas.append((pa, pr))

                        def mmq(e, pa=pa, qk=qk, half=half, sA=sA):
                            ins = None
                            c0 = qk * 256 + half * 128
                            for k in range(16):
                                ins = e.matmul(pa[:, 0:nt], lhsT=WB[:, sA, k, c0:c0 + 128], rhs=HT[:, k, 0:nt],
                                               start=(k == 0), stop=(k == 15))
                            return ins
                        P.op(PE, mmq, reads=r_hTt + [r_wA], writes=[pr])
                    (p1, r1), (p2, r2) = pas
                    cosv, sinv = cs_t[:, 0, 0:nt], cs_t[:, 1, 0:nt]
                    ta, tb, tc, td = (ropetmp[:, i, 0:nt] for i in range(4))
                    P.op(DVE, lambda e, p1=p1, ta=ta, cosv=cosv: e.tensor_tensor(out=ta, in0=p1[:, 0:nt], in1=cosv, op=ALU.mult), reads=[r1, r_cs], writes=[r_rt])
                    P.op(DVE, lambda e, p2=p2, tb=tb, sinv=sinv: e.tensor_tensor(out=tb, in0=p2[:, 0:nt], in1=sinv, op=ALU.mult), reads=[r2, r_cs], writes=[r_rt])
                    P.op(DVE, lambda e, p2=p2, tc=tc, cosv=cosv: e.tensor_tensor(out=tc, in0=p2[:, 0:nt], in1=cosv, op=ALU.mult), reads=[r2, r_cs], writes=[r_rt])
                    P.op(DVE, lambda e, p1=p1, td=td, sinv=sinv: e.tensor_tensor(out=td, in0=p1[:, 0:nt], in1=sinv, op=ALU.mult), reads=[r1, r_cs], writes=[r_rt])
                    P.op(DVE, lambda e, dstT=dstT, ta=ta, tb=tb: e.tensor_tensor(out=dstT[:, 0, 0:nt], in0=ta, in1=tb, op=ALU.subtract), reads=[r_rt], writes=[r_dst])
                    P.op(DVE, lambda e, dstT=dstT, tc=tc, td=td: e.tensor_tensor(out=dstT[:, 1, 0:nt], in0=tc, in1=td, op=ALU.add), reads=[r_rt], writes=[r_dst])
                for tl, (kind, ti) in enumerate(tiles):
                    smp = (ti == 8 and kind == "m")
                    pa, pr = nextps()
                    ncol = 256 if prefix else 512

                    def mmv(e, pa=pa, tl=tl, ncol=ncol, sB=sB):
                        ins = None
                        for k in range(16):
                            ins = e.matmul(pa[:, 0:ncol], lhsT=HT[:, k, tl * 128:(tl + 1) * 128], rhs=WB[:, sB, k, 0:ncol],
                                           start=(k == 0), stop=(k == 15))
                        return ins
                    P.op(PE, mmv, reads=[r_hTt[tl], r_wB], writes=[pr])
                    kd = dec[:, (12 if smp else 8) + h:(12 if smp else 8) + h + 1]
                    P.op(DVE, lambda e, pa=pa, tl=tl, kd=kd: e.tensor_scalar(out=vd[:, tl, :], in0=pa[:, 0:256], scalar1=kd, scalar2=None, op0=ALU.mult),
                         reads=[pr, R["dec"]], writes=[R["vd"]])
                    if not prefix:
                        P.op(ACT, lambda e, pa=pa, tl=tl: e.activation(out=vt[:, tl, :], in_=pa[:, 0:256], func=AF.Copy), reads=[pr], writes=[R["vt"]])
                        P.op(ACT, lambda e, pa=pa, tl=tl: e.activation(out=Gp[:, tl, :], in_=pa[:, 256:512], func=AF.Silu), reads=[pr], writes=[R["Gp"]])
                        P.op(DVE, lambda e, tl=tl, h=h: e.tensor_tensor(out=Gp[:, tl, :], in0=Gp[:, tl, :], in1=wn[:, h * 256:(h + 1) * 256], op=ALU.mult),
                             reads=[R["Gp"], r_wn], writes=[R["Gp"]])
                for tl in range(ntl):
                    pa, pr = nextps()
                    pv = pa.bitcast(BF16)

                    def trk(e, tl=tl, pv=pv):
                        e.transpose(out=pv[:, 0:128], in_=kT[:, 0, tl * 128:(tl + 1) * 128], identity=ident[:])
                        return e.transpose(out=pv[:, 128:256], in_=kT[:, 1, tl * 128:(tl + 1) * 128], identity=ident[:])
                    P.op(PE, trk, reads=[R["kT"], R["ident"]], writes=[pr])
                    P.op(ACT, lambda e, tl=tl, pv=pv: e.activation(out=ktok[:, tl, :], in_=pv[:, 0:256], func=AF.Copy), reads=[pr], writes=[R["ktok"]])
                for fn in deferred:
                    fn()
                deferred = []
                if not prefix:
                    for tl, (kind, ti) in enumerate(tiles):
                        smp = (ti == 8 and kind == "m")
                        tsl = slice(tl * 128, (tl + 1) * 128)
                        pS, rpS = nextps()

                        def mms(e, pS=pS, tsl=tsl):
                            e.matmul(pS[:, 0:128], lhsT=kT[:, 0, tsl], rhs=qT[:, 0, tsl], start=True, stop=False)
                            return e.matmul(pS[:, 0:128], lhsT=kT[:, 1, tsl], rhs=qT[:, 1, tsl], start=False, stop=True)
                        P.op(PE, mms, reads=[R["kT"], R["qT"]], writes=[rpS])
                        mk = mask[:, 1 if smp else 0, h, :]
                        P.op(DVE, lambda e, pS=pS, mk=mk, tl=tl: e.tensor_tensor(out=sTm[:, tl, :], in0=pS[:, 0:128], in1=mk, op=ALU.mult),
                             reads=[rpS, R["mask"]], writes=[R[f"sTm{tl}"]])
                for tl, (kind, ti) in enumerate(tiles):
                    smp = (ti == 8 and kind == "m")
                    if smp:
                        continue
                    pP, rpP = nextps()

                    def mmP(e, pP=pP, tl=tl):
                        e.matmul(pP[:, 0:256], lhsT=ktok[:, tl, 0:128], rhs=vd[:, tl, :], start=True, stop=True)
                        return e.matmul(pP[:, 256:512], lhsT=ktok[:, tl, 128:256], rhs=vd[:, tl, :], start=True, stop=True)
                    P.op(PE, mmP, reads=[R["ktok"], R["vd"]], writes=[rpP])
                    if not prefix:
                        P.op(ACT, lambda e, h=h, tl=tl: e.activation(out=Sbf[:, tl], in_=S[:, h], func=AF.Copy), reads=[rS], writes=[R[f"Sbf{tl}"]])
                    P.op(DVE, lambda e, pP=pP, h=h: e.scalar_tensor_tensor(
                        out=S[:, h].rearrange("p c e -> p (c e)"), in0=S[:, h].rearrange("p c e -> p (c e)"),
                        scalar=float(gam[h] ** 128), in1=pP, op0=ALU.mult, op1=ALU.add),
                        reads=[rpP, rS], writes=[rS])
                    if (kind, ti) == ("m", 7):
                        dma(SP, nrp[h].rearrange("(c p) e -> p c e", p=128), S[:, h], reads=[rS])
                if prefix:
                    continue
                for tl, (kind, ti) in enumerate(tiles):
                    smp = (ti == 8 and kind == "m")
                    tsl = slice(tl * 128, (tl + 1) * 128)
                    i2 = tl % 2
                    pO, rpO = (psum[:, 7, :], R["ps7"]) if smp else nextps()
                    if not smp:
                        def mmoc(e, pO=pO, tsl=tsl, tl=tl):
                            e.matmul(pO[:, 256:512], lhsT=qT[:, 0, tsl], rhs=Sbf[:, tl, 0, :], start=True, stop=False)
                            return e.matmul(pO[:, 256:512], lhsT=qT[:, 1, tsl], rhs=Sbf[:, tl, 1, :], start=False, stop=True)
                        P.op(PE, mmoc, reads=[R["qT"], R[f"Sbf{tl}"]], writes=[rpO])
                    else:
                        for half in range(2):
                            P.op(DVE, lambda e, tl=tl, half=half: e.tensor_tensor(
                                out=kmask_t[:, half], in0=ktok[:, tl, half * 128:(half + 1) * 128].unsqueeze(1).to_broadcast([128, 16, 128]),
                                in1=rowmask[:].unsqueeze(2).to_broadcast([128, 16, 128]), op=ALU.mult),
                                reads=[R["ktok"], R["rowmask"]], writes=[r_km])
                            P.op(DVE, lambda e, tsl=tsl, half=half: e.tensor_tensor(
                                out=qpad_t[:, half], in0=qT[:, half, tsl].unsqueeze(1).to_broadcast([128, 16, 128]),
                                in1=bmask, op=ALU.mult), reads=[R["qT"], r_bm], writes=[r_qp])
                        for b in range(16):
                            n = h * 16 + b
                            while samp["next"] < min(64, n + PF + 1):
                                samp_load(samp["next"])
                                samp["next"] += 1
                            i = n % NSB
                            P.op(ACT, lambda e, i=i: e.activation(out=Sbb_t[i], in_=Sb_t[i], func=AF.Copy),
                                 reads=[r_sb[i]], writes=[r_sbb[i]])

                            def mmsc(e, pO=pO, b=b, i=i):
                                e.matmul(pO[:, 256:512], lhsT=qpad_t[:, 0, b, :], rhs=Sbb_t[i][:, 0:256], start=(b == 0), stop=False)
                                return e.matmul(pO[:, 256:512], lhsT=qpad_t[:, 1, b, :], rhs=Sbb_t[i][:, 256:512], start=False, stop=(b == 15))
                            P.op(PE, mmsc, reads=[r_qp, r_sbb[i]], writes=[rpO])
                            pU, rpU = nextps()

                            def mmsu(e, pU=pU, b=b, tl=tl):
                                e.matmul(pU[:, 0:256], lhsT=kmask_t[:, 0, b, :], rhs=vd[:, tl, :], start=True, stop=True)
                                return e.matmul(pU[:, 256:512], lhsT=kmask_t[:, 1, b, :], rhs=vd[:, tl, :], start=True, stop=True)
                            P.op(PE, mmsu, reads=[r_km, R["vd"]], writes=[rpU])
                            P.op(DVE, lambda e, pU=pU, i=i, h=h: e.scalar_tensor_tensor(
                                out=Sb_t[i], in0=Sb_t[i], scalar=float(gam[h] ** 8), in1=pU, op0=ALU.mult, op1=ALU.add),
                                reads=[rpU, r_sb[i]], writes=[r_sb[i]])
                            dma(SP, nrs[b, h].rearrange("(c p) e -> p c e", p=128), Sb_t[i].rearrange("p (c e) -> p c e", c=2), reads=[r_sb[i]])

                    def mmoi(e, pO=pO, tl=tl):
                        return e.matmul(pO[:, 0:256], lhsT=sTm[:, tl, :], rhs=vt[:, tl, :], start=True, stop=True)
                    P.op(PE, mmoi, reads=[R[f"sTm{tl}"], R["vt"]], writes=[rpO])
                    P.op(ACT, lambda e, pO=pO, i2=i2: e.activation(out=oA[:, i2, :], in_=pO[:, 0:256], func=AF.Copy), reads=[rpO], writes=[R[f"oA{i2}"]])
                    qd = dec[:, (4 if smp else 0) + h:(4 if smp else 0) + h + 1]
                    P.op(DVE, lambda e, pO=pO, i2=i2, qd=qd: e.scalar_tensor_tensor(
                        out=oB[:, i2, :], in0=pO[:, 256:512], scalar=qd, in1=oA[:, i2, :], op0=ALU.mult, op1=ALU.add),
                        reads=[rpO, R[f"oA{i2}"], R["dec"]], writes=[R[f"oB{i2}"]])
                    rs, r_rs = rmsnorm_stats(oB[:, i2, :], R[f"oB{i2}"], 256)
                    P.op(DVE, lambda e, i2=i2, rs=rs, tl=tl: e.scalar_tensor_tensor(
                        out=rr[:, tl, :], in0=oB[:, i2, :], scalar=rs, in1=Gp[:, tl, :], op0=ALU.mult, op1=ALU.mult),
                        reads=[R[f"oB{i2}"], r_rs, R["Gp"]], writes=[R[f"rr{tl}"]])

                    def stage4(tl=tl, h=h, tsl=tsl):
                        pa, pr = nextps()
                        pv = pa.bitcast(BF16)

                        def trr(e, pv=pv, tl=tl):
                            e.transpose(out=pv[:, 0:128], in_=rr[:, tl, 0:128], identity=ident[:])
                            return e.transpose(out=pv[:, 128:256], in_=rr[:, tl, 128:256], identity=ident[:])
                        P.op(PE, trr, reads=[R[f"rr{tl}"], R["ident"]], writes=[pr])
                        P.op(ACT, lambda e, pv=pv, h=h, tsl=tsl: e.activation(
                            out=MT[:, 2 * h:2 * h + 2, tsl], in_=pv[:, 0:256].rearrange("p (c t) -> p c t", c=2), func=AF.Copy),
                            reads=[pr], writes=[r_mT])
                    deferred.append(stage4)
            for fn in deferred:
                fn()
            deferred = []

            ph1 = [r_cs, r_rt, r_wn, r_u, r_hist, r_qp, r_km, r_bm] + r_pA_b + r_pT_b
            if prefix:
                state["arena_parents"] = ph1
                state["hm_parents"] = r_hTt + [r_mT]
                state["x_parents"] = r_x
                return

            if maxphase < 3:
                return
            r_mo = [res(f"mixout{t}", ph1) for t in range(ntl)]
            for cb in range(4):
                s, r_w = load_w([("w_out", 0, D, cb * 512, 512, 0)])
                for tl in range(ntl):
                    pa, pr = nextps()

                    def mmw(e, pa=pa, tl=tl, s=s):
                        ins = None
                        for k in range(16):
                            ins = e.matmul(pa, lhsT=MT[:, k, tl * 128:(tl + 1) * 128], rhs=WB[:, s, k, :], start=(k == 0), stop=(k == 15))
                        return ins
                    P.op(PE, mmw, reads=[r_mT, r_w], writes=[pr])
                    P.op(ACT, lambda e, pa=pa, tl=tl, cb=cb: e.activation(out=mixout[:, tl, cb * 512:(cb + 1) * 512], in_=pa, func=AF.Copy),
                         reads=[pr], writes=[r_mo[tl]])
            r_hfT = res("hfT", r_hTt)
            if has_sample:
                for tl, (kind, ti) in enumerate(tiles):
                    r_x[tl] = res(f"x{tl}", r_sb + r_sbb)
                    dma(SP, X[:, tl, :], xm[ti * 128:(ti + 1) * 128, :], writes=[r_x[tl]])
            for tl in range(ntl):
                rs, r_rs = rmsnorm_stats(mixout[:, tl, :], r_mo[tl], D)
                P.op(DVE, lambda e, tl=tl, rs=rs: e.scalar_tensor_tensor(
                    out=mixout[:, tl, :], in0=mixout[:, tl, :], scalar=rs, in1=bc4[:, 1, :], op0=ALU.mult, op1=ALU.mult),
                    reads=[r_mo[tl], r_rs, R["bc4"]], writes=[r_mo[tl]])
                P.op(DVE, lambda e, tl=tl: e.tensor_tensor(out=X[:, tl, :], in0=X[:, tl, :], in1=mixout[:, tl, :], op=ALU.add),
                     reads=[r_mo[tl], r_x[tl]], writes=[r_x[tl]])
                rs2, r_rs2 = rmsnorm_stats(X[:, tl, :], r_x[tl], D)
                P.op(DVE, lambda e, tl=tl, rs2=rs2: e.scalar_tensor_tensor(
                    out=hbuf[:], in0=X[:, tl, :], scalar=rs2, in1=bc4[:, 2, :], op0=ALU.mult, op1=ALU.mult),
                    reads=[r_x[tl], r_rs2, R["bc4"]], writes=[R["hbuf"]])
                transpose_to(lambda c0, n, tl=tl: HT[:, c0:c0 + n, tl * 128:(tl + 1) * 128], r_hfT, hbuf, R["hbuf"], 16)

            if maxphase < 4:
                return
            r_act = [res(f"act{f}", r_mo) for f in range(NF)]
            sgv = [oA[:].rearrange("p a b -> p (a b)")[:, 0:nt], oB[:].rearrange("p a b -> p (a b)")[:, 0:nt],
                   rr[:].rearrange("p a b -> p (a b)").bitcast(F32)[:, 0:nt],
                   Sbf[:].rearrange("p a c e -> p (a c e)").bitcast(F32)[:, 0:nt]]
            sgr = [[R["oA0"], R["oA1"]], [R["oB0"], R["oB1"]], [R["rr0"], R["rr1"], R["rr2"]], [R["Sbf0"], R["Sbf1"]]]
            for fg in range(NF // 4):
                sg_, r_wg = load_w([("w_gate", 0, D, fg * 512, 512, 0)])
                su_, r_wu = load_w([("w_up", 0, D, fg * 512, 512, 0)])
                for fc in range(4):
                    pg, rpg = nextps()

                    def mmg(e, pg=pg, fc=fc, s=sg_):
                        ins = None
                        for k in range(16):
                            ins = e.matmul(pg[:, 0:nt], lhsT=WB[:, s, k, fc * 128:(fc + 1) * 128], rhs=HT[:, k, 0:nt], start=(k == 0), stop=(k == 15))
                        return ins
                    P.op(PE, mmg, reads=[r_hfT, r_wg], writes=[rpg])
                    P.op(ACT, lambda e, pg=pg, fc=fc: e.activation(out=sgv[fc], in_=pg[:, 0:nt], func=AF.Silu), reads=[rpg], writes=sgr[fc])
                for fc in range(4):
                    f = fg * 4 + fc
                    pu, rpu = nextps()

                    def mmu2(e, pu=pu, fc=fc, s=su_):
                        ins = None
                        for k in range(16):
                            ins = e.matmul(pu[:, 0:nt], lhsT=WB[:, s, k, fc * 128:(fc + 1) * 128], rhs=HT[:, k, 0:nt], start=(k == 0), stop=(k == 15))
                        return ins
                    P.op(PE, mmu2, reads=[r_hfT, r_wu], writes=[rpu])
                    P.op(DVE, lambda e, pu=pu, fc=fc, f=f: e.tensor_tensor(out=actT[:, f, 0:nt], in0=pu[:, 0:nt], in1=sgv[fc], op=ALU.mult),
                         reads=[rpu] + sgr[fc], writes=[r_act[f]])
            r_ff = [res(f"ff{t}", [r_hfT, r_mT] + r_hTt) for t in range(ntl)]
            for cb in range(4):
                pbs = [nextps() for _ in range(ntl)]
                for (k0, nk) in ((0, 16), (16, 16), (32, 12)):
                    s, r_w = load_w([("w_down", k0 * 128, nk * 128, cb * 512, 512, 0)])

                    def mmd(e, k0=k0, nk=nk, s=s, pbs=pbs):
                        ins = None
                        for kk in range(nk):
                            for tl in range(ntl):
                                ins = e.matmul(pbs[tl][0], lhsT=actT[:, k0 + kk, tl * 128:(tl + 1) * 128], rhs=WB[:, s, kk, :],
                                               start=(k0 + kk == 0), stop=(k0 + kk == NF - 1))
                        return ins
                    P.op(PE, mmd, reads=r_act[k0:k0 + nk] + [r_w], writes=[pb[1] for pb in pbs])
                for tl in range(ntl):
                    P.op(ACT, lambda e, tl=tl, cb=cb, pa=pbs[tl][0]: e.activation(out=FFO[:, tl, cb * 512:(cb + 1) * 512], in_=pa, func=AF.Copy),
                         reads=[pbs[tl][1]], writes=[r_ff[tl]])
            for tl, (kind, ti) in enumerate(tiles):
                rs, r_rs = rmsnorm_stats(FFO[:, tl, :], r_ff[tl], D)
                P.op(DVE, lambda e, tl=tl, rs=rs: e.scalar_tensor_tensor(
                    out=FFO[:, tl, :], in0=FFO[:, tl, :], scalar=rs, in1=bc4[:, 3, :], op0=ALU.mult, op1=ALU.mult),
                    reads=[r_ff[tl], r_rs, R["bc4"]], writes=[r_ff[tl]])
                P.op(DVE, lambda e, tl=tl: e.tensor_tensor(out=X[:, tl, :], in0=X[:, tl, :], in1=FFO[:, tl, :], op=ALU.add),
                     reads=[r_ff[tl], r_x[tl]], writes=[r_x[tl]])
                dma(SP, ym[ti * 128:(ti + 1) * 128, :], X[:, tl, :], reads=[r_x[tl]])
            state["arena_parents"] = r_act
            state["hm_parents"] = r_ff
            state["x_parents"] = r_x


        for blk in ([0, 1, 2], [3, 4, 5], [6, 7])[:npre]:
            do_block([("p", i) for i in blk], True)
        for blk in ([0, 1, 2], [3, 4, 5], [6, 7, 8])[:nmain]:
            do_block([("m", i) for i in blk], False)
        P.emit()
        build_nc.stats = P.stats
    return nc


def _consts(half):
    gam = np.array([1.0 - 2.0 ** (-5.0 - h) for h in range(4)], np.float64)
    c = {}
    c["c_ident"] = np.eye(128, dtype=np.float32).astype(ml_dtypes.bfloat16)
    inv_freq = 1.0 / (10000.0 ** (np.arange(0, 256, 2, dtype=np.float64) / 256.0))

    def tab(pos):
        a64 = pos.astype(np.float64)[None, :] * inv_freq[:, None]
        return np.cos(a64).astype(np.float32), np.sin(a64).astype(np.float32)
    posm = np.concatenate([half * 1024 + np.arange(1024), 16384 + (np.arange(128) % 8)]).astype(np.float32)
    c["c_cosm"], c["c_sinm"] = tab(posm)
    c["c_cosp"], c["c_sinp"] = tab(np.arange(1024).astype(np.float32))
    j = np.arange(128)[:, None]; i = np.arange(128)[None, :]
    mask = np.zeros((128, 2, 4, 128), np.float64)
    for h in range(4):
        mp = np.where(i >= j, gam[h] ** np.maximum(i - j, 0), 0.0) / 16.0
        ms = np.where((i >= j) & (i // 8 == j // 8), gam[h] ** np.maximum(i - j, 0), 0.0) / 16.0
        mask[:, 0, h, :] = mp
        mask[:, 1, h, :] = ms
    c["c_mask"] = mask.astype(np.float32)
    dec = np.zeros((128, 16), np.float64)
    t = np.arange(128)
    for h in range(4):
        dec[:, h] = gam[h] ** (t + 1)
        dec[:, 4 + h] = gam[h] ** ((t % 8) + 1)
        dec[:, 8 + h] = gam[h] ** (127 - t) / 16.0
        dec[:, 12 + h] = gam[h] ** (7 - (t % 8)) / 16.0
    c["c_dec"] = dec.astype(np.float32)
    bm = (np.arange(128)[None, :] // 8 == np.arange(16)[:, None]).astype(np.float32)
    c["c_bmask"] = np.broadcast_to(bm[None], (128, 16, 128)).astype(ml_dtypes.bfloat16)
    c["c_rowmask"] = (np.arange(128)[:, None] // 8 == np.arange(16)[None, :]).astype(np.float32)
    A = np.zeros((128, 4, 6, 128), np.float64)
    s = np.arange(128)[:, None]; tt = np.arange(128)[None, :]
    for g, w in enumerate((2, 4, 8, 16)):
        inwin = ((s <= tt) & (s > tt - w)).astype(np.float64)
        eye = (s == tt).astype(np.float64)
        cnt_first = np.minimum(tt + 1, w).astype(np.float64)
        A[:, g, 1] = inwin / w - eye
        A[:, g, 0] = (inwin / cnt_first - eye) if half == 0 else A[:, g, 1]
        A[:, g, 2] = (s >= 128 + tt - w + 1).astype(np.float64) / w
        for q in range(2):
            M = np.zeros((128, 128))
            for bl in range(8):
                b = 8 * q + bl
                for r in range(15):
                    for t8 in range(8):
                        if r >= 16 + t8 - w:
                            M[bl * 15 + r, b * 8 + t8] = 1.0 / w
            A[:, g, 3 + q] = M
        M = np.zeros((128, 128))
        for b in range(16):
            for s8 in range(8):
                for t8 in range(8):
                    v = 0.0
                    if s8 <= t8 and s8 >= t8 - w + 1:
                        v += 1.0 / w
                    if s8 == t8:
                        v -= 1.0
                    M[b * 8 + s8, b * 8 + t8] = v
        A[:, g, 5] = M
    c["c_poolA"] = A.astype(np.float32)
    return c


_CACHE = {}


def make_in_maps(x_prompt, x_sample, state_ret, state_pool, norm_mix_pre, norm_mix_post, w_in, ret_norm_w,
                 w_pool, pool_scale, w_out, norm_ffn_pre, norm_ffn_post, w_gate, w_up, w_down, cores=None):
    f = lambda a: np.ascontiguousarray(np.asarray(a, dtype=np.float32))
    x_prompt, x_sample, state_ret, state_pool = f(x_prompt), f(x_sample), f(state_ret), f(state_pool)
    shared = {
        "w_in": f(w_in), "w_out": f(w_out), "w_gate": f(w_gate), "w_up": f(w_up), "w_down": f(w_down),
        "w_pool": f(w_pool), "nmp": f(norm_mix_pre), "nmpo": f(norm_mix_post), "nfp": f(norm_ffn_pre),
        "nfpo": f(norm_ffn_post), "rnw": f(ret_norm_w), "pscale": f(pool_scale),
    }
    if "consts" not in _CACHE:
        _CACHE["consts"] = [_consts(0), _consts(1)]
    in_maps = []
    for c in (range(NCORES) if cores is None else cores):
        b, half = c // 2, c % 2
        m = dict(shared)
        m.update(_CACHE["consts"][half])
        xs = x_sample[16 * c:16 * (c + 1)].reshape(128, D)
        m["xm"] = np.ascontiguousarray(np.concatenate([x_prompt[b, half * 1024:(half + 1) * 1024], xs], axis=0))
        m["xp"] = np.ascontiguousarray(x_prompt[b, 0:1024]) if half == 1 else np.zeros((1024, D), np.float32)
        m["sret"] = np.ascontiguousarray(state_ret[16 * c:16 * (c + 1)])
        m["spool"] = np.ascontiguousarray(state_pool[16 * c:16 * (c + 1)])
        in_maps.append(m)
    return in_maps


def kernel(**inputs):
    if "nc" not in _CACHE:
        _CACHE["nc"] = build_nc()
    nc = _CACHE["nc"]
    in_maps = make_in_maps(**inputs)
    res = run_bass_kernel_spmd(nc, in_maps, core_ids=list(range(NCORES)))
    outs = res.results
    y_prompt = np.empty((4, 2048, D), np.float32)
    y_sample = np.empty((128, 8, D), np.float32)
    nrp = np.empty((4, 4, 256, 256), np.float32)
    npp = np.empty((4, 15, 1024), np.float32)
    nrs = np.empty((128, 4, 256, 256), np.float32)
    nps = np.empty((128, 15, 1024), np.float32)
    for c in range(NCORES):
        b, half = c // 2, c % 2
        o = outs[c]
        y_prompt[b, half * 1024:(half + 1) * 1024] = o["ym"][0:1024]
        y_sample[16 * c:16 * (c + 1)] = o["ym"][1024:1152].reshape(16, 8, D)
        nrs[16 * c:16 * (c + 1)] = o["nrs"]
        nps[16 * c:16 * (c + 1)] = o["nps"]
        if half == 1:
            nrp[b] = o["nrp"]
            npp[b] = o["npp"]
    return (y_prompt, y_sample, nrp, npp, nrs, nps)
```

```python
import contextlib
import numpy as np
import ml_dtypes
import concourse.bass as bass
import concourse.mybir as mybir
from concourse.bass_utils import run_bass_kernel_spmd

F32 = mybir.dt.float32
BF16 = mybir.dt.bfloat16
AF = mybir.ActivationFunctionType
ALU = mybir.AluOpType

D = 2048
DFF = 5632
NF = DFF // 128
EPS = 1e-6
NCORES = 8
TPB = 3
NT = TPB * 128

PE, ACT, DVE, POOL, SP = "pe", "act", "dve", "pool", "sp"
COMPUTE = (PE, ACT, DVE, POOL)
EPOCH = 12000
import os
USE_WCACHE = True
DBG = {k: True for k in os.environ.get("KDBG", "").split(",") if k}


class Res:
    __slots__ = ("name", "writer", "readers", "excl")

    def __init__(self, name, parents=()):
        self.name = name
        self.excl = name.startswith("ps") and name[2:].isdigit()
        self.writer = None
        self.readers = []
        for p in parents:
            if p.writer is not None:
                self.readers.append(p.writer)
            self.readers.extend(p.readers)


class Op:
    __slots__ = ("eng", "fn", "deps", "is_dma", "signal", "ticket", "slot", "slotval")

    def __init__(self, eng, fn, is_dma):
        self.eng = eng
        self.fn = fn
        self.is_dma = is_dma
        self.deps = []
        self.signal = False
        self.ticket = None
        self.slot = None
        self.slotval = None


class Prog:
    def __init__(self, nc):
        self.nc = nc
        self.ops = {e: [] for e in (PE, ACT, DVE, POOL, SP)}
        self.n_slots = {SP: 16, POOL: 12, ACT: 4}

    def op(self, eng, fn, reads=(), writes=(), dma=False):
        o = Op(eng, fn, dma)
        rset = set()
        deps = []
        for r in reads:
            if r.writer is not None:
                deps.append(r.writer)
                rset.add(id(r.writer))
            if r.excl:
                deps.extend(x for x in r.readers if x.eng != eng)
        for w in writes:
            if w.writer is not None:
                deps.append(w.writer)
            deps.extend(w.readers)
        seen = set()
        for d in deps:
            if id(d) in seen or d is o:
                continue
            seen.add(id(d))
            if (not d.is_dma) and (not dma) and d.eng == eng:
                if eng == PE:
                    continue
            o.deps.append(d)
            d.signal = True
        for r in reads:
            r.readers.append(o)
        for w in writes:
            w.writer = o
            w.readers = []
        self.ops[eng].append(o)
        return o

    def emit(self, final_wait_eng=SP):
        nc = self.nc
        with contextlib.ExitStack() as st:
            counts = {}
            for e in COMPUTE:
                t = 0
                for o in self.ops[e]:
                    if (not o.is_dma) and o.signal:
                        t += 1
                        o.ticket = t
                counts[e] = t
            esems = {}
            for e in COMPUTE:
                n_ep = max(1, (counts[e] + EPOCH - 1) // EPOCH)
                esems[e] = [st.enter_context(nc.semaphore(f"sem_{e}_{i}")) for i in range(n_ep)]
            slot_sems, slot_cnt = {}, {}
            for e in (SP, POOL, ACT):
                dmas = [o for o in self.ops[e] if o.is_dma]
                if not dmas:
                    continue
                ns = self.n_slots[e]
                slot_sems[e] = [st.enter_context(nc.semaphore(f"dsem_{e}_{i}")) for i in range(ns)]
                slot_cnt[e] = [0] * ns
                for i, o in enumerate(dmas):
                    s = i % ns
                    slot_cnt[e][s] += 1
                    o.slot = (e, s)
                    o.slotval = 16 * slot_cnt[e][s]
            self.stats = {e: (len(self.ops[e]), counts.get(e, 0)) for e in self.ops}
            block = st.enter_context(nc.Block())
            engs = {PE: block.tensor, ACT: block.scalar, DVE: block.vector, POOL: block.gpsimd, SP: block.sync}

            def make(e):
                def body(eng):
                    waited = {}

                    def wait(key, sem, val):
                        if waited.get(key, 0) >= val:
                            return
                        waited[key] = val
                        eng.wait_ge(sem, val)

                    for o in self.ops[e]:
                        for d in o.deps:
                            if d.is_dma:
                                de, ds = d.slot
                                wait(("d", de, ds), slot_sems[de][ds], d.slotval)
                            else:
                                ep = (d.ticket - 1) // EPOCH
                                wait(("c", d.eng, ep), esems[d.eng][ep], d.ticket - ep * EPOCH)
                        if o.is_dma:
                            de, ds = o.slot
                            if o.slotval > 16:
                                wait(("d", de, ds), slot_sems[de][ds], o.slotval - 16)
                            o.fn(eng).then_inc(slot_sems[de][ds], 16)
                        else:
                            ins = o.fn(eng)
                            if o.signal:
                                ep = (o.ticket - 1) // EPOCH
                                ins.then_inc(esems[e][ep], 1)
                    if e == final_wait_eng:
                        for de in slot_sems:
                            for s, sem in enumerate(slot_sems[de]):
                                if slot_cnt[de][s] > 0:
                                    eng.wait_ge(sem, 16 * slot_cnt[de][s])
                return body

            for e in (PE, ACT, DVE, POOL, SP):
                if self.ops[e] or e == final_wait_eng:
                    engs[e](make(e))


def build_nc(npre=3, nmain=3, maxphase=9):
    nc = bass.Bass("TRN2", target_bir_lowering=False)

    def din(name, shape, dt=F32):
        return nc.dram_tensor(name, list(shape), dt, kind="ExternalInput").ap()

    def dout(name, shape, dt=F32):
        return nc.dram_tensor(name, list(shape), dt, kind="ExternalOutput").ap()

    xm = din("xm", [1152, D]); xp = din("xp", [1024, D])
    sret = din("sret", [16, 4, 256, 256]); spool = din("spool", [16, 15, 1024])
    w_in = din("w_in", [D, 5120]); w_out = din("w_out", [D, D])
    w_gate = din("w_gate", [D, DFF]); w_up = din("w_up", [D, DFF]); w_down = din("w_down", [DFF, D])
    w_pool = din("w_pool", [4, 256, 256])
    nvec = [din(n, [D]) for n in ("nmp", "nmpo", "nfp", "nfpo")]
    rnw = din("rnw", [1024]); pscale = din("pscale", [1024])
    c_ident = din("c_ident", [128, 128], BF16)
    c_cosm = din("c_cosm", [128, 1152]); c_sinm = din("c_sinm", [128, 1152])
    c_cosp = din("c_cosp", [128, 1024]); c_sinp = din("c_sinp", [128, 1024])
    c_mask = din("c_mask", [128, 2, 4, 128])
    c_dec = din("c_dec", [128, 16])
    c_bmask = din("c_bmask", [128, 16, 128], BF16)
    c_rowmask = din("c_rowmask", [128, 16])
    c_poolA = din("c_poolA", [128, 4, 6, 128])

    def dscr(name, shape, dt):
        return nc.dram_tensor(name, list(shape), dt, kind="Internal").ap()
    WC_TOTAL = D * 5120 + D * D + 3 * D * DFF
    wc = dscr("wc", [WC_TOTAL], BF16)
    ym = dout("ym", [1152, D]); nrp = dout("nrp", [4, 256, 256]); npp = dout("npp", [15, 1024])
    nrs = dout("nrs", [16, 4, 256, 256]); nps = dout("nps", [16, 15, 1024])

    gam = [1.0 - 2.0 ** (-5.0 - h) for h in range(4)]

    with contextlib.ExitStack() as st:
        def sb(name, shape, dt):
            return st.enter_context(nc.sbuf_tensor(name, list(shape), dt))

        ident = sb("ident", [128, 128], BF16)
        bc4 = sb("bc4", [128, 4, D], F32)
        S = sb("S", [128, 4, 2, 256], F32)
        psc = sb("psc", [128, 8], F32)
        wpool = sb("wpool", [128, 4, 2, 256], BF16)
        mask = sb("mask", [128, 2, 4, 128], F32)
        dec = sb("dec", [128, 16], F32)
        rowmask = sb("rowmask", [128, 16], F32)
        stat = sb("stat", [128, 64], F32)
        hbuf = sb("hbuf", [128, D], BF16)
        junk = hbuf
        X = sb("X", [128, TPB, D], F32)
        Xf = X[:].rearrange("p t f -> p (t f)")
        NSB = 8
        Sb_t = [Xf[:, i * 768:i * 768 + 512] for i in range(NSB)]
        Sbb_t = [Xf[:, i * 768 + 512:i * 768 + 768].bitcast(BF16) for i in range(NSB)]
        HM = sb("HM", [128, 2, 16, NT], BF16)
        HT = HM[:, 0]
        MT = HM[:, 1]
        FFO = HM[:].rearrange("p a k t -> p (a k t)").bitcast(F32).rearrange("p (t f) -> p t f", t=TPB)
        ACTA = sb("ACTA", [128, NF * NT], BF16)
        actT = ACTA[:].rearrange("p (f t) -> p f t", f=NF)
        A32 = ACTA[:].bitcast(F32)
        mixout = A32[:, 0:TPB * D].rearrange("p (t f) -> p t f", t=TPB)
        WB = sb("WB", [128, 3, 16, 512], BF16)
        o_ = [0]

        def carve(n_f32):
            a = o_[0]
            o_[0] += n_f32
            return A32[:, a:a + n_f32]
        cs_t = carve(2 * NT).rearrange("p (a t) -> p a t", a=2)
        ropetmp = carve(4 * NT).rearrange("p (a t) -> p a t", a=4)
        wn = carve(1024)
        pool_base = o_[0]
        u_t = carve(4 * 512).rearrange("p (s c) -> p s c", s=4)
        hist_t = carve(2 * 512).rearrange("p (q c) -> p q c", q=2)
        poolA_b = [carve(6 * 128).rearrange("p (k t) -> p k t", k=6) for _ in range(2)]
        pooledT_b = [carve(NT).bitcast(BF16).rearrange("p (c t) -> p c t", c=2),
                     A32[:, 2 * NT:3 * NT].bitcast(BF16).rearrange("p (c t) -> p c t", c=2)]
        pool_end = o_[0]
        o_[0] = pool_base
        qpad_t = carve(2048).bitcast(BF16).rearrange("p (a b m) -> p a b m", a=2, b=16)
        kmask_t = carve(2048).bitcast(BF16).rearrange("p (a b d) -> p a b d", a=2, b=16)
        bmask = carve(1024).bitcast(BF16).rearrange("p (b m) -> p b m", b=16)
        o_[0] = max(o_[0], pool_end)
        assert o_[0] <= NF * NT // 2, o_[0]
        qT = sb("qT", [128, 2, NT], BF16); kT = sb("kT", [128, 2, NT], BF16)
        ktok = sb("ktok", [128, TPB, 256], BF16)
        vt = sb("vt", [128, TPB, 256], BF16); vd = sb("vd", [128, TPB, 256], BF16)
        Gp = sb("Gp", [128, TPB, 256], F32)
        oA = sb("oA", [128, 2, 256], F32); oB = sb("oB", [128, 2, 256], F32)
        sTm = sb("sTm", [128, 3, 128], BF16); rr = sb("rr", [128, 3, 256], BF16)
        Sbf = sb("Sbf", [128, 3, 2, 256], BF16)
        uprev = sb("uprev", [128, 1024], F32)
        psum = st.enter_context(nc.psum_tensor("psum", [128, 8, 512], F32))

        P = Prog(nc)
        build_nc.sbuf_left = nc.sbuf_bytes_remaining
        R = {}

        def res(name, parents=()):
            R[name] = Res(name, parents)
            return R[name]

        for n in ["ident", "bc4", "S0", "S1", "S2", "S3", "psc", "wpool", "mask", "dec", "rowmask",
                  "hbuf", "uprev", "Sbf", "qT", "kT", "ktok", "vt", "vd", "Gp"]:
            res(n)
        for i in range(2):
            res(f"oA{i}"); res(f"oB{i}")
        for i in range(3):
            res(f"sTm{i}"); res(f"rr{i}"); res(f"Sbf{i}")

        for i in range(8):
            res(f"ps{i}")
        for i in range(3):
            res(f"wb{i}")
        for i in range(64):
            res(f"stat{i}")
        psi = [0]

        def nextps():
            i = psi[0] % 7
            psi[0] += 1
            return psum[:, i, :], R[f"ps{i}"]
        sti = [0]

        def nextstat():
            i = sti[0] % 64
            sti[0] += 1
            return stat[:, i:i + 1], R[f"stat{i}"]
        wbi = [0]

        def dma(eng, out, in_, reads=(), writes=(), **kw):
            return P.op(eng, lambda e: e.dma_start(out=out, in_=in_, **kw), reads=reads, writes=writes, dma=True)

        dma(SP, ident[:], c_ident[:, :], writes=[R["ident"]])
        for i in range(4):
            dma(SP, bc4[:, i, :], nvec[i].partition_broadcast(128), writes=[R["bc4"]])
        dma(SP, psc[:], pscale.rearrange("(c p) -> p c", p=128), writes=[R["psc"]], allow_slow_non_contiguous=True)
        dma(SP, mask[:], c_mask[:, :, :, :], writes=[R["mask"]])
        dma(SP, dec[:], c_dec[:, :], writes=[R["dec"]])
        dma(SP, rowmask[:], c_rowmask[:, :], writes=[R["rowmask"]])
        dma(POOL, wpool[:], w_pool.rearrange("g (c p) d -> p g c d", p=128), writes=[R["wpool"]])
        for h in range(4):
            P.op(DVE, lambda e, h=h: e.memset(S[:, h], 0.0), writes=[R[f"S{h}"]])
        P.op(DVE, lambda e: e.memset(uprev[:], 0.0), writes=[R["uprev"]])

        WSRC = {"w_in": w_in, "w_out": w_out, "w_gate": w_gate, "w_up": w_up, "w_down": w_down}
        wcache = {}
        wc_off = [0]
        wcount = {}
        cache_on = {"w_in": 0, "w_out": 0, "w_gate": 0, "w_up": 1, "w_down": 1}

        def load_w(pieces):
            s = wbi[0] % 3
            wbi[0] += 1
            r = R[f"wb{s}"]
            for (wname, r0, nr, c0, ncol, coff) in pieces:
                nk = nr // 128
                key = (wname, r0, nr, c0, ncol)
                dst = WB[:, s, 0:nk, coff:coff + ncol]
                wf = WSRC[wname]
                if USE_WCACHE and key in wcache:
                    rc, off = wcache[key]
                    dma(POOL, dst, wc[off:off + nr * ncol].rearrange("(p k n) -> p k n", p=128, k=nk), reads=[rc], writes=[r])
                else:
                    dma(POOL, dst, wf[r0:r0 + nr, c0:c0 + ncol].rearrange("(k p) n -> p k n", p=128), writes=[r])
                    cnt = wcount.get(key, 0)
                    wcount[key] = cnt + 1
                    if USE_WCACHE and cnt == cache_on[wname]:
                        off = wc_off[0]
                        wc_off[0] += nr * ncol
                        assert wc_off[0] <= WC_TOTAL
                        wcache[key] = (Res("wc"), off)
                        dma(SP, wc[off:off + nr * ncol].rearrange("(p k n) -> p k n", p=128, k=nk), dst, reads=[r], writes=[wcache[key][0]])
            return s, r

        def rmsnorm_stats(src_ap, src_res, n):
            ss, r_ss = nextstat()
            P.op(ACT, lambda e: e.activation(out=junk[:, 0:n], in_=src_ap, func=AF.Square, accum_out=ss),
                 reads=[src_res], writes=[R["hbuf"], r_ss])
            sq, r_sq = nextstat()
            P.op(ACT, lambda e: e.activation(out=sq, in_=ss, func=AF.Sqrt, scale=1.0 / n, bias=EPS),
                 reads=[r_ss], writes=[r_sq])
            rs, r_rs = nextstat()
            P.op(DVE, lambda e: e.reciprocal(out=rs, in_=sq), reads=[r_sq], writes=[r_rs])
            return rs, r_rs

        def transpose_to(dst_fn, dst_res, src_tile_ap, src_res, nchunks):
            for c0 in range(0, nchunks, 4):
                n = min(4, nchunks - c0)
                pa, pr = nextps()
                pv = pa.bitcast(BF16)

                def tr(e, c0=c0, n=n, pv=pv):
                    ins = None
                    for j in range(n):
                        ins = e.transpose(out=pv[:, j * 128:(j + 1) * 128],
                                          in_=src_tile_ap[:, (c0 + j) * 128:(c0 + j + 1) * 128], identity=ident[:])
                    return ins
                P.op(PE, tr, reads=[src_res, R["ident"]], writes=[pr])
                P.op(ACT, lambda e, c0=c0, n=n, pv=pv: e.activation(
                    out=dst_fn(c0, n), in_=pv[:, 0:n * 128].rearrange("p (j t) -> p j t", j=n), func=AF.Copy),
                    reads=[pr], writes=[dst_res])


        state = {"arena_parents": [], "hm_parents": [], "x_parents": []}

        def do_block(tiles, prefix):
            ntl = len(tiles)
            nt = ntl * 128
            has_sample = any(k == "m" and i == 8 for k, i in tiles)
            r_x = [res(f"x{t}", state["x_parents"]) for t in range(ntl)]
            r_hTt = [res(f"hT{t}", state["hm_parents"]) for t in range(ntl)]
            r_mT = res("mT", state["hm_parents"])
            ap_par = state["arena_parents"]
            r_cs = res("cs", ap_par); r_rt = res("ropetmp", ap_par); r_wn = res("wn", ap_par)
            r_u = res("u", ap_par); r_hist = res("hist", ap_par)
            r_pA_b = [res(f"poolA{i}", ap_par) for i in range(2)]
            r_pT_b = [res("pooledT0", ap_par), r_rt]
            for tl, (kind, ti) in enumerate(tiles):
                src = (xp if kind == "p" else xm)[ti * 128:(ti + 1) * 128, :]
                dma(SP, X[:, tl, :], src, writes=[r_x[tl]])
            cosd, sind = (c_cosp, c_sinp) if prefix else (c_cosm, c_sinm)
            t0 = tiles[0][1] * 128
            dma(SP, cs_t[:, 0, 0:nt], cosd[:, t0:t0 + nt], writes=[r_cs])
            dma(SP, cs_t[:, 1, 0:nt], sind[:, t0:t0 + nt], writes=[r_cs])
            if not prefix:
                dma(SP, wn, rnw.partition_broadcast(128), writes=[r_wn])
            for tl in range(ntl):
                rs, r_rs = rmsnorm_stats(X[:, tl, :], r_x[tl], D)
                P.op(DVE, lambda e, tl=tl, rs=rs: e.scalar_tensor_tensor(
                    out=hbuf[:], in0=X[:, tl, :], scalar=rs, in1=bc4[:, 0, :], op0=ALU.mult, op1=ALU.mult),
                    reads=[r_x[tl], r_rs, R["bc4"]], writes=[R["hbuf"]])
                transpose_to(lambda c0, n, tl=tl: HT[:, c0:c0 + n, tl * 128:(tl + 1) * 128], r_hTt[tl], hbuf, R["hbuf"], 16)

            if maxphase < 1:
                return
            need_u = (not prefix) or tiles[-1] == ("p", 7)
            if need_u:
                for j in range(2):
                    s, r_w = load_w([("w_in", 0, D, 4096 + 512 * j, 512, 0)])
                    if not prefix:
                        P.op(DVE, lambda e, j=j: e.tensor_copy(out=u_t[:, 0, :], in_=uprev[:, j * 512:(j + 1) * 512]),
                             reads=[R["uprev"]], writes=[r_u])
                    for tl in range(ntl):
                        if prefix and tl != ntl - 1:
                            continue
                        pa, pr = nextps()

                        def mmu(e, tl=tl, s=s, pa=pa):
                            ins = None
                            for k in range(16):
                                ins = e.matmul(pa, lhsT=HT[:, k, tl * 128:(tl + 1) * 128], rhs=WB[:, s, k, :],
                                               start=(k == 0), stop=(k == 15))
                            return ins
                        P.op(PE, mmu, reads=[r_hTt[tl], r_w], writes=[pr])
                        if prefix:
                            P.op(ACT, lambda e, pa=pa, j=j: e.activation(out=uprev[:, j * 512:(j + 1) * 512], in_=pa, func=AF.Copy),
                                 reads=[pr], writes=[R["uprev"]])
                        else:
                            P.op(ACT, lambda e, pa=pa, tl=tl: e.activation(out=u_t[:, tl + 1, :], in_=pa, func=AF.Copy),
                                 reads=[pr], writes=[r_u])
                    if prefix:
                        continue
                    for tl, (kind, ti) in enumerate(tiles):
                        if ti == 8:
                            dma(SP, hist_t[0:120, 0, :], spool[0:8, :, j * 512:(j + 1) * 512].rearrange("b r c -> (b r) c"), writes=[r_hist])
                            dma(SP, hist_t[0:120, 1, :], spool[8:16, :, j * 512:(j + 1) * 512].rearrange("b r c -> (b r) c"), writes=[r_hist])
                    for gg in range(2):
                        g = 2 * j + gg
                        poolA_t, r_pA = poolA_b[g % 2], r_pA_b[g % 2]
                        pooledT, r_pT = pooledT_b[g % 2], r_pT_b[g % 2]
                        dma(SP, poolA_t, c_poolA[:, g, :, :], writes=[r_pA])
                        for cc in range(2):
                            c0 = gg * 256 + cc * 128
                            pa, pr = nextps()

                            def mmp(e, pa=pa, c0=c0, poolA_t=poolA_t):
                                ins = None
                                for tl, (kind, ti) in enumerate(tiles):
                                    o = pa[:, tl * 128:(tl + 1) * 128]
                                    if ti == 8:
                                        e.matmul(o, lhsT=hist_t[0:120, 0, c0:c0 + 128], rhs=poolA_t[0:120, 3, :], start=True, stop=False)
                                        e.matmul(o, lhsT=hist_t[0:120, 1, c0:c0 + 128], rhs=poolA_t[0:120, 4, :], start=False, stop=False)
                                        ins = e.matmul(o, lhsT=u_t[:, tl + 1, c0:c0 + 128], rhs=poolA_t[:, 5, :], start=False, stop=True)
                                    else:
                                        e.matmul(o, lhsT=u_t[:, tl, c0:c0 + 128], rhs=poolA_t[:, 2, :], start=True, stop=False)
                                        ins = e.matmul(o, lhsT=u_t[:, tl + 1, c0:c0 + 128], rhs=poolA_t[:, 0 if ti == 0 else 1, :],
                                                       start=False, stop=True)
                                return ins
                            P.op(PE, mmp, reads=[r_u, r_hist, r_pA], writes=[pr])
                            P.op(ACT, lambda e, pa=pa, cc=cc, pooledT=pooledT: e.activation(out=pooledT[:, cc, 0:nt], in_=pa[:, 0:nt], func=AF.Copy),
                                 reads=[pr], writes=[r_pT])
                        for dc in range(2):
                            pa, pr = nextps()

                            def mmo(e, pa=pa, g=g, dc=dc, pooledT=pooledT):
                                e.matmul(pa[:, 0:nt], lhsT=wpool[:, g, 0, dc * 128:(dc + 1) * 128], rhs=pooledT[:, 0, 0:nt], start=True, stop=False)
                                return e.matmul(pa[:, 0:nt], lhsT=wpool[:, g, 1, dc * 128:(dc + 1) * 128], rhs=pooledT[:, 1, 0:nt], start=False, stop=True)
                            P.op(PE, mmo, reads=[r_pT, R["wpool"]], writes=[pr])
                            P.op(ACT, lambda e, pa=pa, g=g, dc=dc: e.activation(
                                out=MT[:, 8 + g * 2 + dc, 0:nt], in_=pa[:, 0:nt], func=AF.Copy, scale=psc[:, g * 2 + dc:g * 2 + dc + 1]),
                                reads=[pr, R["psc"]], writes=[r_mT])
                    for tl, (kind, ti) in enumerate(tiles):
                        if ti == 7:
                            dma(SP, npp[:, j * 512:(j + 1) * 512], u_t[113:128, tl + 1, :], reads=[r_u])
                        if ti == 8:
                            for b in range(16):
                                dma(SP, nps[b, 7:15, j * 512:(j + 1) * 512], u_t[b * 8:(b + 1) * 8, tl + 1, :], reads=[r_u])
                            for q in range(2):
                                for bl in range(8):
                                    dma(SP, nps[8 * q + bl, 0:7, j * 512:(j + 1) * 512], hist_t[bl * 15 + 8:bl * 15 + 15, q, :], reads=[r_hist])
                    last_tl = max(tl for tl, (kind, ti) in enumerate(tiles) if ti != 8)
                    P.op(DVE, lambda e, j=j, last_tl=last_tl: e.tensor_copy(out=uprev[:, j * 512:(j + 1) * 512], in_=u_t[:, last_tl + 1, :]),
                         reads=[r_u], writes=[R["uprev"]])

            if maxphase < 2:
                return
            samp_par = [r_u, r_hist] + r_pA_b + r_pT_b
            r_bm = res("bmask", samp_par)
            r_sb = [res(f"sb{i}", r_x) for i in range(NSB)] if has_sample else []
            r_sbb = [res(f"sbb{i}", r_x) for i in range(NSB)] if has_sample else []
            samp = {"next": 0}
            PF = 5

            def samp_load(n):
                hh, bb = divmod(n, 16)
                i = n % NSB
                dma(SP, Sb_t[i].rearrange("p (c e) -> p c e", c=2), sret[bb, hh].rearrange("(c p) e -> p c e", p=128), writes=[r_sb[i]])
            if has_sample:
                dma(SP, bmask, c_bmask[:, :, :], writes=[r_bm])
            r_qp = res("qpad", samp_par); r_km = res("kmask", samp_par)
            sbi = [0]
            deferred = []
            for h in range(4):
                rS = R[f"S{h}"]
                def head_weights(hh):
                    srcs = [("w_in", 0, D, 1024 + hh * 256, 256, 256)]
                    if not prefix:
                        srcs = [("w_in", 0, D, hh * 256, 256, 0)] + srcs
                    a = load_w(srcs)
                    srcs = [("w_in", 0, D, 2048 + hh * 256, 256, 0)]
                    if not prefix:
                        srcs.append(("w_in", 0, D, 3072 + hh * 256, 256, 256))
                    return a + load_w(srcs)
                if h == 0:
                    hw_next = head_weights(0)
                sA, r_wA, sB, r_wB = hw_next
                for qk in ([1] if prefix else [0, 1]):
                    dstT, r_dst = (qT, R["qT"]) if qk == 0 else (kT, R["kT"])
                    pas = []
                    for half in range(2):
                        pa, pr = nextps()
                        pas.append((pa, pr))

                        def mmq(e, pa=pa, qk=qk, half=half, sA=sA):
                            ins = None
                            c0 = qk * 256 + half * 128
                            for k in range(16):
                                ins = e.matmul(pa[:, 0:nt], lhsT=WB[:, sA, k, c0:c0 + 128], rhs=HT[:, k, 0:nt],
                                               start=(k == 0), stop=(k == 15))
                            return ins
                        P.op(PE, mmq, reads=r_hTt + [r_wA], writes=[pr])
                    (p1, r1), (p2, r2) = pas
                    cosv, sinv = cs_t[:, 0, 0:nt], cs_t[:, 1, 0:nt]
                    ta, tb, tc, td = (ropetmp[:, i, 0:nt] for i in range(4))
                    P.op(DVE, lambda e, p1=p1, ta=ta, cosv=cosv: e.tensor_tensor(out=ta, in0=p1[:, 0:nt], in1=cosv, op=ALU.mult), reads=[r1, r_cs], writes=[r_rt])
                    P.op(DVE, lambda e, p2=p2, tb=tb, sinv=sinv: e.tensor_tensor(out=tb, in0=p2[:, 0:nt], in1=sinv, op=ALU.mult), reads=[r2, r_cs], writes=[r_rt])
                    P.op(DVE, lambda e, p2=p2, tc=tc, cosv=cosv: e.tensor_tensor(out=tc, in0=p2[:, 0:nt], in1=cosv, op=ALU.mult), reads=[r2, r_cs], writes=[r_rt])
                    P.op(DVE, lambda e, p1=p1, td=td, sinv=sinv: e.tensor_tensor(out=td, in0=p1[:, 0:nt], in1=sinv, op=ALU.mult), reads=[r1, r_cs], writes=[r_rt])
                    P.op(DVE, lambda e, dstT=dstT, ta=ta, tb=tb: e.tensor_tensor(out=dstT[:, 0, 0:nt], in0=ta, in1=tb, op=ALU.subtract), reads=[r_rt], writes=[r_dst])
                    P.op(DVE, lambda e, dstT=dstT, tc=tc, td=td: e.tensor_tensor(out=dstT[:, 1, 0:nt], in0=tc, in1=td, op=ALU.add), reads=[r_rt], writes=[r_dst])
                for tl, (kind, ti) in enumerate(tiles):
                    smp = (ti == 8 and kind == "m")
                    pa, pr = nextps()
                    ncol = 256 if prefix else 512

                    def mmv(e, pa=pa, tl=tl, ncol=ncol, sB=sB):
                        ins = None
                        for k in range(16):
                            ins = e.matmul(pa[:, 0:ncol], lhsT=HT[:, k, tl * 128:(tl + 1) * 128], rhs=WB[:, sB, k, 0:ncol],
                                           start=(k == 0), stop=(k == 15))
                        return ins
                    P.op(PE, mmv, reads=[r_hTt[tl], r_wB], writes=[pr])
                    kd = dec[:, (12 if smp else 8) + h:(12 if smp else 8) + h + 1]
                    P.op(DVE, lambda e, pa=pa, tl=tl, kd=kd: e.tensor_scalar(out=vd[:, tl, :], in0=pa[:, 0:256], scalar1=kd, scalar2=None, op0=ALU.mult),
                         reads=[pr, R["dec"]], writes=[R["vd"]])
                    if not prefix:
                        P.op(ACT, lambda e, pa=pa, tl=tl: e.activation(out=vt[:, tl, :], in_=pa[:, 0:256], func=AF.Copy), reads=[pr], writes=[R["vt"]])
                        P.op(ACT, lambda e, pa=pa, tl=tl: e.activation(out=Gp[:, tl, :], in_=pa[:, 256:512], func=AF.Silu), reads=[pr], writes=[R["Gp"]])
                        P.op(DVE, lambda e, tl=tl, h=h: e.tensor_tensor(out=Gp[:, tl, :], in0=Gp[:, tl, :], in1=wn[:, h * 256:(h + 1) * 256], op=ALU.mult),
                             reads=[R["Gp"], r_wn], writes=[R["Gp"]])
                for tl in range(ntl):
                    pa, pr = nextps()
                    pv = pa.bitcast(BF16)

                    def trk(e, tl=tl, pv=pv):
                        e.transpose(out=pv[:, 0:128], in_=kT[:, 0, tl * 128:(tl + 1) * 128], identity=ident[:])
                        return e.transpose(out=pv[:, 128:256], in_=kT[:, 1, tl * 128:(tl + 1) * 128], identity=ident[:])
                    P.op(PE, trk, reads=[R["kT"], R["ident"]], writes=[pr])
                    P.op(ACT, lambda e, tl=tl, pv=pv: e.activation(out=ktok[:, tl, :], in_=pv[:, 0:256], func=AF.Copy), reads=[pr], writes=[R["ktok"]])
                if h < 3:
                    hw_next = head_weights(h + 1)
                for fn in deferred:
                    fn()
                deferred = []
                if not prefix:
                    for tl, (kind, ti) in enumerate(tiles):
                        smp = (ti == 8 and kind == "m")
                        tsl = slice(tl * 128, (tl + 1) * 128)
                        pS, rpS = nextps()

                        def mms(e, pS=pS, tsl=tsl):
                            e.matmul(pS[:, 0:128], lhsT=kT[:, 0, tsl], rhs=qT[:, 0, tsl], start=True, stop=False)
                            return e.matmul(pS[:, 0:128], lhsT=kT[:, 1, tsl], rhs=qT[:, 1, tsl], start=False, stop=True)
                        P.op(PE, mms, reads=[R["kT"], R["qT"]], writes=[rpS])
                        mk = mask[:, 1 if smp else 0, h, :]
                        P.op(DVE, lambda e, pS=pS, mk=mk, tl=tl: e.tensor_tensor(out=sTm[:, tl, :], in0=pS[:, 0:128], in1=mk, op=ALU.mult),
                             reads=[rpS, R["mask"]], writes=[R[f"sTm{tl}"]])
                for tl, (kind, ti) in enumerate(tiles):
                    smp = (ti == 8 and kind == "m")
                    if smp:
                        continue
                    pP, rpP = nextps()

                    def mmP(e, pP=pP, tl=tl):
                        e.matmul(pP[:, 0:256], lhsT=ktok[:, tl, 0:128], rhs=vd[:, tl, :], start=True, stop=True)
                        return e.matmul(pP[:, 256:512], lhsT=ktok[:, tl, 128:256], rhs=vd[:, tl, :], start=True, stop=True)
                    P.op(PE, mmP, reads=[R["ktok"], R["vd"]], writes=[rpP])
                    if not prefix:
                        P.op(ACT, lambda e, h=h, tl=tl: e.activation(out=Sbf[:, tl], in_=S[:, h], func=AF.Copy), reads=[rS], writes=[R[f"Sbf{tl}"]])
                    P.op(DVE, lambda e, pP=pP, h=h: e.scalar_tensor_tensor(
                        out=S[:, h].rearrange("p c e -> p (c e)"), in0=S[:, h].rearrange("p c e -> p (c e)"),
                        scalar=float(gam[h] ** 128), in1=pP, op0=ALU.mult, op1=ALU.add),
                        reads=[rpP, rS], writes=[rS])
                    if (kind, ti) == ("m", 7):
                        dma(SP, nrp[h].rearrange("(c p) e -> p c e", p=128), S[:, h], reads=[rS])
                if prefix:
                    continue
                for tl, (kind, ti) in enumerate(tiles):
                    smp = (ti == 8 and kind == "m")
                    tsl = slice(tl * 128, (tl + 1) * 128)
                    i2 = tl % 2
                    pO, rpO = (psum[:, 7, :], R["ps7"]) if smp else nextps()
                    if not smp:
                        def mmoc(e, pO=pO, tsl=tsl, tl=tl):
                            e.matmul(pO[:, 256:512], lhsT=qT[:, 0, tsl], rhs=Sbf[:, tl, 0, :], start=True, stop=False)
                            return e.matmul(pO[:, 256:512], lhsT=qT[:, 1, tsl], rhs=Sbf[:, tl, 1, :], start=False, stop=True)
                        P.op(PE, mmoc, reads=[R["qT"], R[f"Sbf{tl}"]], writes=[rpO])
                    else:
                        for half in range(2):
                            P.op(DVE, lambda e, tl=tl, half=half: e.tensor_tensor(
                                out=kmask_t[:, half], in0=ktok[:, tl, half * 128:(half + 1) * 128].unsqueeze(1).to_broadcast([128, 16, 128]),
                                in1=rowmask[:].unsqueeze(2).to_broadcast([128, 16, 128]), op=ALU.mult),
                                reads=[R["ktok"], R["rowmask"]], writes=[r_km])
                            P.op(DVE, lambda e, tsl=tsl, half=half: e.tensor_tensor(
                                out=qpad_t[:, half], in0=qT[:, half, tsl].unsqueeze(1).to_broadcast([128, 16, 128]),
                                in1=bmask, op=ALU.mult), reads=[R["qT"], r_bm], writes=[r_qp])
                        for b in range(16):
                            n = h * 16 + b
                            while samp["next"] < min(64, n + PF + 1):
                                samp_load(samp["next"])
                                samp["next"] += 1
                            i = n % NSB
                            P.op(ACT, lambda e, i=i: e.activation(out=Sbb_t[i], in_=Sb_t[i], func=AF.Copy),
                                 reads=[r_sb[i]], writes=[r_sbb[i]])

                            def mmsc(e, pO=pO, b=b, i=i):
                                e.matmul(pO[:, 256:512], lhsT=qpad_t[:, 0, b, :], rhs=Sbb_t[i][:, 0:256], start=(b == 0), stop=False)
                                return e.matmul(pO[:, 256:512], lhsT=qpad_t[:, 1, b, :], rhs=Sbb_t[i][:, 256:512], start=False, stop=(b == 15))
                            P.op(PE, mmsc, reads=[r_qp, r_sbb[i]], writes=[rpO])
                            pU, rpU = nextps()

                            def mmsu(e, pU=pU, b=b, tl=tl):
                                e.matmul(pU[:, 0:256], lhsT=kmask_t[:, 0, b, :], rhs=vd[:, tl, :], start=True, stop=True)
                                return e.matmul(pU[:, 256:512], lhsT=kmask_t[:, 1, b, :], rhs=vd[:, tl, :], start=True, stop=True)
                            P.op(PE, mmsu, reads=[r_km, R["vd"]], writes=[rpU])
                            P.op(DVE, lambda e, pU=pU, i=i, h=h: e.scalar_tensor_tensor(
                                out=Sb_t[i], in0=Sb_t[i], scalar=float(gam[h] ** 8), in1=pU, op0=ALU.mult, op1=ALU.add),
                                reads=[rpU, r_sb[i]], writes=[r_sb[i]])
                            dma(POOL, nrs[b, h].rearrange("(c p) e -> p c e", p=128), Sb_t[i].rearrange("p (c e) -> p c e", c=2), reads=[r_sb[i]])

                    def mmoi(e, pO=pO, tl=tl):
                        return e.matmul(pO[:, 0:256], lhsT=sTm[:, tl, :], rhs=vt[:, tl, :], start=True, stop=True)
                    P.op(PE, mmoi, reads=[R[f"sTm{tl}"], R["vt"]], writes=[rpO])
                    P.op(ACT, lambda e, pO=pO, i2=i2: e.activation(out=oA[:, i2, :], in_=pO[:, 0:256], func=AF.Copy), reads=[rpO], writes=[R[f"oA{i2}"]])
                    qd = dec[:, (4 if smp else 0) + h:(4 if smp else 0) + h + 1]
                    P.op(DVE, lambda e, pO=pO, i2=i2, qd=qd: e.scalar_tensor_tensor(
                        out=oB[:, i2, :], in0=pO[:, 256:512], scalar=qd, in1=oA[:, i2, :], op0=ALU.mult, op1=ALU.add),
                        reads=[rpO, R[f"oA{i2}"], R["dec"]], writes=[R[f"oB{i2}"]])
                    rs, r_rs = rmsnorm_stats(oB[:, i2, :], R[f"oB{i2}"], 256)
                    P.op(DVE, lambda e, i2=i2, rs=rs, tl=tl: e.scalar_tensor_tensor(
                        out=rr[:, tl, :], in0=oB[:, i2, :], scalar=rs, in1=Gp[:, tl, :], op0=ALU.mult, op1=ALU.mult),
                        reads=[R[f"oB{i2}"], r_rs, R["Gp"]], writes=[R[f"rr{tl}"]])

                    def stage4(tl=tl, h=h, tsl=tsl):
                        pa, pr = nextps()
                        pv = pa.bitcast(BF16)

                        def trr(e, pv=pv, tl=tl):
                            e.transpose(out=pv[:, 0:128], in_=rr[:, tl, 0:128], identity=ident[:])
                            return e.transpose(out=pv[:, 128:256], in_=rr[:, tl, 128:256], identity=ident[:])
                        P.op(PE, trr, reads=[R[f"rr{tl}"], R["ident"]], writes=[pr])
                        P.op(ACT, lambda e, pv=pv, h=h, tsl=tsl: e.activation(
                            out=MT[:, 2 * h:2 * h + 2, tsl], in_=pv[:, 0:256].rearrange("p (c t) -> p c t", c=2), func=AF.Copy),
                            reads=[pr], writes=[r_mT])
                    deferred.append(stage4)
            for fn in deferred:
                fn()
            deferred = []

            ph1 = [r_cs, r_rt, r_wn, r_u, r_hist, r_qp, r_km, r_bm] + r_pA_b + r_pT_b
            if prefix:
                state["arena_parents"] = ph1
                state["hm_parents"] = r_hTt + [r_mT]
                state["x_parents"] = r_x
                return

            if maxphase < 3:
                return
            r_mo = [res(f"mixout{t}", ph1) for t in range(ntl)]
            for cb in range(4):
                s, r_w = load_w([("w_out", 0, D, cb * 512, 512, 0)])
                for tl in range(ntl):
                    pa, pr = nextps()

                    def mmw(e, pa=pa, tl=tl, s=s):
                        ins = None
                        for k in range(16):
                            ins = e.matmul(pa, lhsT=MT[:, k, tl * 128:(tl + 1) * 128], rhs=WB[:, s, k, :], start=(k == 0), stop=(k == 15))
                        return ins
                    P.op(PE, mmw, reads=[r_mT, r_w], writes=[pr])
                    P.op(ACT, lambda e, pa=pa, tl=tl, cb=cb: e.activation(out=mixout[:, tl, cb * 512:(cb + 1) * 512], in_=pa, func=AF.Copy),
                         reads=[pr], writes=[r_mo[tl]])
            r_hfT = res("hfT", r_hTt)
            if has_sample:
                for tl, (kind, ti) in enumerate(tiles):
                    r_x[tl] = res(f"x{tl}", r_sb + r_sbb)
                    dma(SP, X[:, tl, :], xm[ti * 128:(ti + 1) * 128, :], writes=[r_x[tl]])
            for tl in range(ntl):
                rs, r_rs = rmsnorm_stats(mixout[:, tl, :], r_mo[tl], D)
                P.op(DVE, lambda e, tl=tl, rs=rs: e.scalar_tensor_tensor(
                    out=mixout[:, tl, :], in0=mixout[:, tl, :], scalar=rs, in1=bc4[:, 1, :], op0=ALU.mult, op1=ALU.mult),
                    reads=[r_mo[tl], r_rs, R["bc4"]], writes=[r_mo[tl]])
                P.op(DVE, lambda e, tl=tl: e.tensor_tensor(out=X[:, tl, :], in0=X[:, tl, :], in1=mixout[:, tl, :], op=ALU.add),
                     reads=[r_mo[tl], r_x[tl]], writes=[r_x[tl]])
                rs2, r_rs2 = rmsnorm_stats(X[:, tl, :], r_x[tl], D)
                P.op(DVE, lambda e, tl=tl, rs2=rs2: e.scalar_tensor_tensor(
                    out=hbuf[:], in0=X[:, tl, :], scalar=rs2, in1=bc4[:, 2, :], op0=ALU.mult, op1=ALU.mult),
                    reads=[r_x[tl], r_rs2, R["bc4"]], writes=[R["hbuf"]])
                transpose_to(lambda c0, n, tl=tl: HT[:, c0:c0 + n, tl * 128:(tl + 1) * 128], r_hfT, hbuf, R["hbuf"], 16)

            if maxphase < 4:
                return
            r_act = [res(f"act{f}", r_mo) for f in range(NF)]
            sgv = [oA[:].rearrange("p a b -> p (a b)")[:, 0:nt], oB[:].rearrange("p a b -> p (a b)")[:, 0:nt],
                   rr[:].rearrange("p a b -> p (a b)").bitcast(F32)[:, 0:nt],
                   Sbf[:].rearrange("p a c e -> p (a c e)").bitcast(F32)[:, 0:nt]]
            sgr = [[R["oA0"], R["oA1"]], [R["oB0"], R["oB1"]], [R["rr0"], R["rr1"], R["rr2"]], [R["Sbf0"], R["Sbf1"]]]
            for fg in range(NF // 4):
                sg_, r_wg = load_w([("w_gate", 0, D, fg * 512, 512, 0)])
                su_, r_wu = load_w([("w_up", 0, D, fg * 512, 512, 0)])
                for fc in range(4):
                    pg, rpg = nextps()

                    def mmg(e, pg=pg, fc=fc, s=sg_):
                        ins = None
                        for k in range(16):
                            ins = e.matmul(pg[:, 0:nt], lhsT=WB[:, s, k, fc * 128:(fc + 1) * 128], rhs=HT[:, k, 0:nt], start=(k == 0), stop=(k == 15))
                        return ins
                    P.op(PE, mmg, reads=[r_hfT, r_wg], writes=[rpg])
                    P.op(ACT, lambda e, pg=pg, fc=fc: e.activation(out=sgv[fc], in_=pg[:, 0:nt], func=AF.Silu), reads=[rpg], writes=sgr[fc])
                for fc in range(4):
                    f = fg * 4 + fc
                    pu, rpu = nextps()

                    def mmu2(e, pu=pu, fc=fc, s=su_):
                        ins = None
                        for k in range(16):
                            ins = e.matmul(pu[:, 0:nt], lhsT=WB[:, s, k, fc * 128:(fc + 1) * 128], rhs=HT[:, k, 0:nt], start=(k == 0), stop=(k == 15))
                        return ins
                    P.op(PE, mmu2, reads=[r_hfT, r_wu], writes=[rpu])
                    P.op(DVE, lambda e, pu=pu, fc=fc, f=f: e.tensor_tensor(out=actT[:, f, 0:nt], in0=pu[:, 0:nt], in1=sgv[fc], op=ALU.mult),
                         reads=[rpu] + sgr[fc], writes=[r_act[f]])
            r_ff = [res(f"ff{t}", [r_hfT, r_mT] + r_hTt) for t in range(ntl)]
            for cb in range(4):
                pbs = [nextps() for _ in range(ntl)]
                for (k0, nk) in ((0, 16), (16, 16), (32, 12)):
                    s, r_w = load_w([("w_down", k0 * 128, nk * 128, cb * 512, 512, 0)])

                    def mmd(e, k0=k0, nk=nk, s=s, pbs=pbs):
                        ins = None
                        for kk in range(nk):
                            for tl in range(ntl):
                                ins = e.matmul(pbs[tl][0], lhsT=actT[:, k0 + kk, tl * 128:(tl + 1) * 128], rhs=WB[:, s, kk, :],
                                               start=(k0 + kk == 0), stop=(k0 + kk == NF - 1))
                        return ins
                    P.op(PE, mmd, reads=r_act[k0:k0 + nk] + [r_w], writes=[pb[1] for pb in pbs])
                for tl in range(ntl):
                    P.op(ACT, lambda e, tl=tl, cb=cb, pa=pbs[tl][0]: e.activation(out=FFO[:, tl, cb * 512:(cb + 1) * 512], in_=pa, func=AF.Copy),
                         reads=[pbs[tl][1]], writes=[r_ff[tl]])
            for tl, (kind, ti) in enumerate(tiles):
                rs, r_rs = rmsnorm_stats(FFO[:, tl, :], r_ff[tl], D)
                P.op(DVE, lambda e, tl=tl, rs=rs: e.scalar_tensor_tensor(
                    out=FFO[:, tl, :], in0=FFO[:, tl, :], scalar=rs, in1=bc4[:, 3, :], op0=ALU.mult, op1=ALU.mult),
                    reads=[r_ff[tl], r_rs, R["bc4"]], writes=[r_ff[tl]])
                P.op(DVE, lambda e, tl=tl: e.tensor_tensor(out=X[:, tl, :], in0=X[:, tl, :], in1=FFO[:, tl, :], op=ALU.add),
                     reads=[r_ff[tl], r_x[tl]], writes=[r_x[tl]])
                dma(SP, ym[ti * 128:(ti + 1) * 128, :], X[:, tl, :], reads=[r_x[tl]])
            state["arena_parents"] = r_act
            state["hm_parents"] = r_ff
            state["x_parents"] = r_x


        for blk in ([0, 1, 2], [3, 4, 5], [6, 7])[:npre]:
            do_block([("p", i) for i in blk], True)
        for blk in ([0, 1, 2], [3, 4, 5], [6, 7, 8])[:nmain]:
            do_block([("m", i) for i in blk], False)
        P.emit()
        build_nc.stats = P.stats
    return nc


def _consts(half):
    gam = np.array([1.0 - 2.0 ** (-5.0 - h) for h in range(4)], np.float64)
    c = {}
    c["c_ident"] = np.eye(128, dtype=np.float32).astype(ml_dtypes.bfloat16)
    inv_freq = 1.0 / (10000.0 ** (np.arange(0, 256, 2, dtype=np.float64) / 256.0))

    def tab(pos):
        a64 = pos.astype(np.float64)[None, :] * inv_freq[:, None]
        return np.cos(a64).astype(np.float32), np.sin(a64).astype(np.float32)
    posm = np.concatenate([half * 1024 + np.arange(1024), 16384 + (np.arange(128) % 8)]).astype(np.float32)
    c["c_cosm"], c["c_sinm"] = tab(posm)
    c["c_cosp"], c["c_sinp"] = tab(np.arange(1024).astype(np.float32))
    j = np.arange(128)[:, None]; i = np.arange(128)[None, :]
    mask = np.zeros((128, 2, 4, 128), np.float64)
    for h in range(4):
        mp = np.where(i >= j, gam[h] ** np.maximum(i - j, 0), 0.0) / 16.0
        ms = np.where((i >= j) & (i // 8 == j // 8), gam[h] ** np.maximum(i - j, 0), 0.0) / 16.0
        mask[:, 0, h, :] = mp
        mask[:, 1, h, :] = ms
    c["c_mask"] = mask.astype(np.float32)
    dec = np.zeros((128, 16), np.float64)
    t = np.arange(128)
    for h in range(4):
        dec[:, h] = gam[h] ** (t + 1)
        dec[:, 4 + h] = gam[h] ** ((t % 8) + 1)
        dec[:, 8 + h] = gam[h] ** (127 - t) / 16.0
        dec[:, 12 + h] = gam[h] ** (7 - (t % 8)) / 16.0
    c["c_dec"] = dec.astype(np.float32)
    bm = (np.arange(128)[None, :] // 8 == np.arange(16)[:, None]).astype(np.float32)
    c["c_bmask"] = np.broadcast_to(bm[None], (128, 16, 128)).astype(ml_dtypes.bfloat16)
    c["c_rowmask"] = (np.arange(128)[:, None] // 8 == np.arange(16)[None, :]).astype(np.float32)
    A = np.zeros((128, 4, 6, 128), np.float64)
    s = np.arange(128)[:, None]; tt = np.arange(128)[None, :]
    for g, w in enumerate((2, 4, 8, 16)):
        inwin = ((s <= tt) & (s > tt - w)).astype(np.float64)
        eye = (s == tt).astype(np.float64)
        cnt_first = np.minimum(tt + 1, w).astype(np.float64)
        A[:, g, 1] = inwin / w - eye
        A[:, g, 0] = (inwin / cnt_first - eye) if half == 0 else A[:, g, 1]
        A[:, g, 2] = (s >= 128 + tt - w + 1).astype(np.float64) / w
        for q in range(2):
            M = np.zeros((128, 128))
            for bl in range(8):
                b = 8 * q + bl
                for r in range(15):
                    for t8 in range(8):
                        if r >= 16 + t8 - w:
                            M[bl * 15 + r, b * 8 + t8] = 1.0 / w
            A[:, g, 3 + q] = M
        M = np.zeros((128, 128))
        for b in range(16):
            for s8 in range(8):
                for t8 in range(8):
                    v = 0.0
                    if s8 <= t8 and s8 >= t8 - w + 1:
                        v += 1.0 / w
                    if s8 == t8:
                        v -= 1.0
                    M[b * 8 + s8, b * 8 + t8] = v
        A[:, g, 5] = M
    c["c_poolA"] = A.astype(np.float32)
    return c


_CACHE = {}


def make_in_maps(x_prompt, x_sample, state_ret, state_pool, norm_mix_pre, norm_mix_post, w_in, ret_norm_w,
                 w_pool, pool_scale, w_out, norm_ffn_pre, norm_ffn_post, w_gate, w_up, w_down, cores=None):
    f = lambda a: np.ascontiguousarray(np.asarray(a, dtype=np.float32))
    x_prompt, x_sample, state_ret, state_pool = f(x_prompt), f(x_sample), f(state_ret), f(state_pool)
    shared = {
        "w_in": f(w_in), "w_out": f(w_out), "w_gate": f(w_gate), "w_up": f(w_up), "w_down": f(w_down),
        "w_pool": f(w_pool), "nmp": f(norm_mix_pre), "nmpo": f(norm_mix_post), "nfp": f(norm_ffn_pre),
        "nfpo": f(norm_ffn_post), "rnw": f(ret_norm_w), "pscale": f(pool_scale),
    }
    if "consts" not in _CACHE:
        _CACHE["consts"] = [_consts(0), _consts(1)]
    in_maps = []
    for c in (range(NCORES) if cores is None else cores):
        b, half = c // 2, c % 2
        m = dict(shared)
        m.update(_CACHE["consts"][half])
        xs = x_sample[16 * c:16 * (c + 1)].reshape(128, D)
        m["xm"] = np.ascontiguousarray(np.concatenate([x_prompt[b, half * 1024:(half + 1) * 1024], xs], axis=0))
        m["xp"] = np.ascontiguousarray(x_prompt[b, 0:1024]) if half == 1 else np.zeros((1024, D), np.float32)
        m["sret"] = np.ascontiguousarray(state_ret[16 * c:16 * (c + 1)])
        m["spool"] = np.ascontiguousarray(state_pool[16 * c:16 * (c + 1)])
        in_maps.append(m)
    return in_maps


def kernel(**inputs):
    if "nc" not in _CACHE:
        _CACHE["nc"] = build_nc()
    nc = _CACHE["nc"]
    in_maps = make_in_maps(**inputs)
    res = run_bass_kernel_spmd(nc, in_maps, core_ids=list(range(NCORES)))
    outs = res.results
    y_prompt = np.empty((4, 2048, D), np.float32)
    y_sample = np.empty((128, 8, D), np.float32)
    nrp = np.empty((4, 4, 256, 256), np.float32)
    npp = np.empty((4, 15, 1024), np.float32)
    nrs = np.empty((128, 4, 256, 256), np.float32)
    nps = np.empty((128, 15, 1024), np.float32)
    for c in range(NCORES):
        b, half = c // 2, c % 2
        o = outs[c]
        y_prompt[b, half * 1024:(half + 1) * 1024] = o["ym"][0:1024]
        y_sample[16 * c:16 * (c + 1)] = o["ym"][1024:1152].reshape(16, 8, D)
        nrs[16 * c:16 * (c + 1)] = o["nrs"]
        nps[16 * c:16 * (c + 1)] = o["nps"]
        if half == 1:
            nrp[b] = o["nrp"]
            npp[b] = o["npp"]
    return (y_prompt, y_sample, nrp, npp, nrs, nps)
```

```python
import contextlib
import numpy as np
import ml_dtypes
import concourse.bass as bass
import concourse.mybir as mybir
from concourse.bass_utils import run_bass_kernel_spmd

F32 = mybir.dt.float32
BF16 = mybir.dt.bfloat16
AF = mybir.ActivationFunctionType
ALU = mybir.AluOpType

D = 2048
DFF = 5632
NF = DFF // 128
EPS = 1e-6
NCORES = 8
TPB = 3
NT = TPB * 128

PE, ACT, DVE, POOL, SP = "pe", "act", "dve", "pool", "sp"
COMPUTE = (PE, ACT, DVE, POOL)
EPOCH = 12000
import os
USE_WCACHE = True
DBG = {k: True for k in os.environ.get("KDBG", "").split(",") if k}


class Res:
    __slots__ = ("name", "writer", "readers", "excl")

    def __init__(self, name, parents=()):
        self.name = name
        self.excl = name.startswith("ps") and name[2:].isdigit()
        self.writer = None
        self.readers = []
        for p in parents:
            if p.writer is not None:
                self.readers.append(p.writer)
            self.readers.extend(p.readers)


class Op:
    __slots__ = ("eng", "fn", "deps", "is_dma", "signal", "ticket", "slot", "slotval")

    def __init__(self, eng, fn, is_dma):
        self.eng = eng
        self.fn = fn
        self.is_dma = is_dma
        self.deps = []
        self.signal = False
        self.ticket = None
        self.slot = None
        self.slotval = None


class Prog:
    def __init__(self, nc):
        self.nc = nc
        self.ops = {e: [] for e in (PE, ACT, DVE, POOL, SP)}
        self.n_slots = {SP: 16, POOL: 12, ACT: 4}

    def op(self, eng, fn, reads=(), writes=(), dma=False):
        o = Op(eng, fn, dma)
        rset = set()
        deps = []
        for r in reads:
            if r.writer is not None:
                deps.append(r.writer)
                rset.add(id(r.writer))
            if r.excl:
                deps.extend(x for x in r.readers if x.eng != eng)
        for w in writes:
            if w.writer is not None:
                deps.append(w.writer)
            deps.extend(w.readers)
        seen = set()
        for d in deps:
            if id(d) in seen or d is o:
                continue
            seen.add(id(d))
            if (not d.is_dma) and (not dma) and d.eng == eng:
                if eng == PE:
                    continue
            o.deps.append(d)
            d.signal = True
        for r in reads:
            r.readers.append(o)
        for w in writes:
            w.writer = o
            w.readers = []
        self.ops[eng].append(o)
        return o

    def emit(self, final_wait_eng=SP):
        nc = self.nc
        with contextlib.ExitStack() as st:
            counts = {}
            for e in COMPUTE:
                t = 0
                for o in self.ops[e]:
                    if (not o.is_dma) and o.signal:
                        t += 1
                        o.ticket = t
                counts[e] = t
            esems = {}
            for e in COMPUTE:
                n_ep = max(1, (counts[e] + EPOCH - 1) // EPOCH)
                esems[e] = [st.enter_context(nc.semaphore(f"sem_{e}_{i}")) for i in range(n_ep)]
            slot_sems, slot_cnt = {}, {}
            for e in (SP, POOL, ACT):
                dmas = [o for o in self.ops[e] if o.is_dma]
                if not dmas:
                    continue
                ns = self.n_slots[e]
                slot_sems[e] = [st.enter_context(nc.semaphore(f"dsem_{e}_{i}")) for i in range(ns)]
                slot_cnt[e] = [0] * ns
                for i, o in enumerate(dmas):
                    s = i % ns
                    slot_cnt[e][s] += 1
                    o.slot = (e, s)
                    o.slotval = 16 * slot_cnt[e][s]
            self.stats = {e: (len(self.ops[e]), counts.get(e, 0)) for e in self.ops}
            block = st.enter_context(nc.Block())
            engs = {PE: block.tensor, ACT: block.scalar, DVE: block.vector, POOL: block.gpsimd, SP: block.sync}

            def make(e):
                def body(eng):
                    waited = {}

                    def wait(key, sem, val):
                        if waited.get(key, 0) >= val:
                            return
                        waited[key] = val
                        eng.wait_ge(sem, val)

                    for o in self.ops[e]:
                        for d in o.deps:
                            if d.is_dma:
                                de, ds = d.slot
                                wait(("d", de, ds), slot_sems[de][ds], d.slotval)
                            else:
                                ep = (d.ticket - 1) // EPOCH
                                wait(("c", d.eng, ep), esems[d.eng][ep], d.ticket - ep * EPOCH)
                        if o.is_dma:
                            de, ds = o.slot
                            if o.slotval > 16:
                                wait(("d", de, ds), slot_sems[de][ds], o.slotval - 16)
                            o.fn(eng).then_inc(slot_sems[de][ds], 16)
                        else:
                            ins = o.fn(eng)
                            if o.signal:
                                ep = (o.ticket - 1) // EPOCH
                                ins.then_inc(esems[e][ep], 1)
                    if e == final_wait_eng:
                        for de in slot_sems:
                            for s, sem in enumerate(slot_sems[de]):
                                if slot_cnt[de][s] > 0:
                                    eng.wait_ge(sem, 16 * slot_cnt[de][s])
                return body

            for e in (PE, ACT, DVE, POOL, SP):
                if self.ops[e] or e == final_wait_eng:
                    engs[e](make(e))


def build_nc():
    nc = bass.Bass("TRN2", target_bir_lowering=False)

    def din(name, shape, dt=F32):
        return nc.dram_tensor(name, list(shape), dt, kind="ExternalInput").ap()

    def dout(name, shape, dt=F32):
        return nc.dram_tensor(name, list(shape), dt, kind="ExternalOutput").ap()

    xm = din("xm", [1152, D]); xp = din("xp", [1024, D])
    sret = din("sret", [16, 4, 256, 256]); spool = din("spool", [16, 15, 1024])
    w_in = din("w_in", [D, 5120]); w_out = din("w_out", [D, D])
    w_gate = din("w_gate", [D, DFF]); w_up = din("w_up", [D, DFF]); w_down = din("w_down", [DFF, D])
    w_pool = din("w_pool", [4, 256, 256])
    nvec = [din(n, [D]) for n in ("nmp", "nmpo", "nfp", "nfpo")]
    rnw = din("rnw", [1024]); pscale = din("pscale", [1024])
    c_ident = din("c_ident", [128, 128], BF16)
    c_cosm = din("c_cosm", [128, 1152]); c_sinm = din("c_sinm", [128, 1152])
    c_cosp = din("c_cosp", [128, 1024]); c_sinp = din("c_sinp", [128, 1024])
    c_mask = din("c_mask", [128, 2, 4, 128])
    c_dec = din("c_dec", [128, 16])
    c_bmask = din("c_bmask", [128, 16, 128], BF16)
    c_rowmask = din("c_rowmask", [128, 16])
    c_poolA = din("c_poolA", [128, 4, 6, 128])

    def dscr(name, shape, dt):
        return nc.dram_tensor(name, list(shape), dt, kind="Internal").ap()
    WC_TOTAL = D * 5120 + D * D + 3 * D * DFF
    wc = dscr("wc", [WC_TOTAL], BF16)
    ym = dout("ym", [1152, D]); nrp = dout("nrp", [4, 256, 256]); npp = dout("npp", [15, 1024])
    nrs = dout("nrs", [16, 4, 256, 256]); nps = dout("nps", [16, 15, 1024])

    gam = [1.0 - 2.0 ** (-5.0 - h) for h in range(4)]

    with contextlib.ExitStack() as st:
        def sb(name, shape, dt):
            return st.enter_context(nc.sbuf_tensor(name, list(shape), dt))

        ident = sb("ident", [128, 128], BF16)
        bc4 = sb("bc4", [128, 4, D], F32)
        S = sb("S", [128, 4, 2, 256], F32)
        psc = sb("psc", [128, 8], F32)
        wpool = sb("wpool", [128, 4, 2, 256], BF16)
        mask = sb("mask", [128, 2, 4, 128], F32)
        dec = sb("dec", [128, 16], F32)
        rowmask = sb("rowmask", [128, 16], F32)
        stat = sb("stat", [128, 64], F32)
        hbuf = sb("hbuf", [128, D], BF16)
        junk = hbuf
        X = sb("X", [128, TPB, D], F32)
        Xf = X[:].rearrange("p t f -> p (t f)")
        NSB = 8
        Sb_t = [Xf[:, i * 768:i * 768 + 512] for i in range(NSB)]
        Sbb_t = [Xf[:, i * 768 + 512:i * 768 + 768].bitcast(BF16) for i in range(NSB)]
        HM = sb("HM", [128, 2, 16, NT], BF16)
        HT = HM[:, 0]
        MT = HM[:, 1]
        FFO = HM[:].rearrange("p a k t -> p (a k t)").bitcast(F32).rearrange("p (t f) -> p t f", t=TPB)
        ACTA = sb("ACTA", [128, NF * NT], BF16)
        actT = ACTA[:].rearrange("p (f t) -> p f t", f=NF)
        A32 = ACTA[:].bitcast(F32)
        mixout = A32[:, 0:TPB * D].rearrange("p (t f) -> p t f", t=TPB)
        WB = sb("WB", [128, 3, 16, 512], BF16)
        o_ = [0]

        def carve(n_f32):
            a = o_[0]
            o_[0] += n_f32
            return A32[:, a:a + n_f32]
        cs_t = carve(2 * NT).rearrange("p (a t) -> p a t", a=2)
        ropetmp = carve(4 * NT).rearrange("p (a t) -> p a t", a=4)
        wn = carve(1024)
        pool_base = o_[0]
        u_t = carve(4 * 512).rearrange("p (s c) -> p s c", s=4)
        hist_t = carve(2 * 512).rearrange("p (q c) -> p q c", q=2)
        poolA_b = [carve(6 * 128).rearrange("p (k t) -> p k t", k=6) for _ in range(2)]
        pooledT_b = [carve(NT).bitcast(BF16).rearrange("p (c t) -> p c t", c=2),
                     A32[:, 2 * NT:3 * NT].bitcast(BF16).rearrange("p (c t) -> p c t", c=2)]
        pool_end = o_[0]
        o_[0] = pool_base
        qpad_t = carve(2048).bitcast(BF16).rearrange("p (a b m) -> p a b m", a=2, b=16)
        kmask_t = carve(2048).bitcast(BF16).rearrange("p (a b d) -> p a b d", a=2, b=16)
        bmask = carve(1024).bitcast(BF16).rearrange("p (b m) -> p b m", b=16)
        o_[0] = max(o_[0], pool_end)
        assert o_[0] <= NF * NT // 2, o_[0]
        qT = sb("qT", [128, 2, NT], BF16); kT = sb("kT", [128, 2, NT], BF16)
        ktok = sb("ktok", [128, TPB, 256], BF16)
        vt = sb("vt", [128, TPB, 256], BF16); vd = sb("vd", [128, TPB, 256], BF16)
        Gp = sb("Gp", [128, TPB, 256], F32)
        oA = sb("oA", [128, 2, 256], F32); oB = sb("oB", [128, 2, 256], F32)
        sTm = sb("sTm", [128, 3, 128], BF16); rr = sb("rr", [128, 3, 256], BF16)
        Sbf = sb("Sbf", [128, 3, 2, 256], BF16)
        uprev = sb("uprev", [128, 1024], F32)
        psum = st.enter_context(nc.psum_tensor("psum", [128, 8, 512], F32))

        P = Prog(nc)
        build_nc.sbuf_left = nc.sbuf_bytes_remaining
        R = {}

        def res(name, parents=()):
            R[name] = Res(name, parents)
            return R[name]

        for n in ["ident", "bc4", "S0", "S1", "S2", "S3", "psc", "wpool", "mask", "dec", "rowmask",
                  "hbuf", "uprev", "Sbf", "qT", "kT", "ktok", "vt", "vd", "Gp"]:
            res(n)
        for i in range(2):
            res(f"oA{i}"); res(f"oB{i}")
        for i in range(3):
            res(f"sTm{i}"); res(f"rr{i}"); res(f"Sbf{i}")

        for i in range(8):
            res(f"ps{i}")
        for i in range(3):
            res(f"wb{i}")
        for i in range(64):
            res(f"stat{i}")
        psi = [0]

        def nextps():
            i = psi[0] % 7
            psi[0] += 1
            return psum[:, i, :], R[f"ps{i}"]
        sti = [0]

        def nextstat():
            i = sti[0] % 64
            sti[0] += 1
            return stat[:, i:i + 1], R[f"stat{i}"]
        wbi = [0]

        def dma(eng, out, in_, reads=(), writes=(), **kw):
            return P.op(eng, lambda e: e.dma_start(out=out, in_=in_, **kw), reads=reads, writes=writes, dma=True)

        dma(SP, ident[:], c_ident[:, :], writes=[R["ident"]])
        for i in range(4):
            dma(SP, bc4[:, i, :], nvec[i].partition_broadcast(128), writes=[R["bc4"]])
        dma(SP, psc[:], pscale.rearrange("(c p) -> p c", p=128), writes=[R["psc"]], allow_slow_non_contiguous=True)
        dma(SP, mask[:], c_mask[:, :, :, :], writes=[R["mask"]])
        dma(SP, dec[:], c_dec[:, :], writes=[R["dec"]])
        dma(SP, rowmask[:], c_rowmask[:, :], writes=[R["rowmask"]])
        dma(POOL, wpool[:], w_pool.rearrange("g (c p) d -> p g c d", p=128), writes=[R["wpool"]])
        for h in range(4):
            P.op(DVE, lambda e, h=h: e.memset(S[:, h], 0.0), writes=[R[f"S{h}"]])
        P.op(DVE, lambda e: e.memset(uprev[:], 0.0), writes=[R["uprev"]])

        WSRC = {"w_in": w_in, "w_out": w_out, "w_gate": w_gate, "w_up": w_up, "w_down": w_down}
        wcache = {}
        wc_off = [0]
        wcount = {}
        cache_on = {"w_in": 0, "w_out": 0, "w_gate": 0, "w_up": 1, "w_down": 1}

        def load_w(pieces):
            s = wbi[0] % 3
            wbi[0] += 1
            r = R[f"wb{s}"]
            for (wname, r0, nr, c0, ncol, coff) in pieces:
                nk = nr // 128
                key = (wname, r0, nr, c0, ncol)
                dst = WB[:, s, 0:nk, coff:coff + ncol]
                wf = WSRC[wname]
                if USE_WCACHE and key in wcache:
                    rc, off = wcache[key]
                    dma(POOL, dst, wc[off:off + nr * ncol].rearrange("(p k n) -> p k n", p=128, k=nk), reads=[rc], writes=[r])
                else:
                    dma(POOL, dst, wf[r0:r0 + nr, c0:c0 + ncol].rearrange("(k p) n -> p k n", p=128), writes=[r])
                    cnt = wcount.get(key, 0)
                    wcount[key] = cnt + 1
                    if USE_WCACHE and cnt == cache_on[wname]:
                        off = wc_off[0]
                        wc_off[0] += nr * ncol
                        assert wc_off[0] <= WC_TOTAL
                        wcache[key] = (Res("wc"), off)
                        dma(SP, wc[off:off + nr * ncol].rearrange("(p k n) -> p k n", p=128, k=nk), dst, reads=[r], writes=[wcache[key][0]])
            return s, r

        def rmsnorm_stats(src_ap, src_res, n):
            ss, r_ss = nextstat()
            P.op(ACT, lambda e: e.activation(out=junk[:, 0:n], in_=src_ap, func=AF.Square, accum_out=ss),
                 reads=[src_res], writes=[R["hbuf"], r_ss])
            sq, r_sq = nextstat()
            P.op(ACT, lambda e: e.activation(out=sq, in_=ss, func=AF.Sqrt, scale=1.0 / n, bias=EPS),
                 reads=[r_ss], writes=[r_sq])
            rs, r_rs = nextstat()
            P.op(DVE, lambda e: e.reciprocal(out=rs, in_=sq), reads=[r_sq], writes=[r_rs])
            return rs, r_rs

        def transpose_to(dst_fn, dst_res, src_tile_ap, src_res, nchunks):
            for c0 in range(0, nchunks, 4):
                n = min(4, nchunks - c0)
                pa, pr = nextps()
                pv = pa.bitcast(BF16)

                def tr(e, c0=c0, n=n, pv=pv):
                    ins = None
                    for j in range(n):
                        ins = e.transpose(out=pv[:, j * 128:(j + 1) * 128],
                                          in_=src_tile_ap[:, (c0 + j) * 128:(c0 + j + 1) * 128], identity=ident[:])
                    return ins
                P.op(PE, tr, reads=[src_res, R["ident"]], writes=[pr])
                P.op(ACT, lambda e, c0=c0, n=n, pv=pv: e.activation(
                    out=dst_fn(c0, n), in_=pv[:, 0:n * 128].rearrange("p (j t) -> p j t", j=n), func=AF.Copy),
                    reads=[pr], writes=[dst_res])


        state = {"arena_parents": [], "hm_par": [[], []], "x_parents": []}

        def phase0(tiles, prefix, buf):
            ntl = len(tiles)
            HTv = HM[:, buf]
            r_x = [res(f"x{t}", state["x_parents"]) for t in range(ntl)]
            r_hTt = [res(f"hT{t}", state["hm_par"][buf]) for t in range(ntl)]
            for tl, (kind, ti) in enumerate(tiles):
                src = (xp if kind == "p" else xm)[ti * 128:(ti + 1) * 128, :]
                dma(SP, X[:, tl, :], src, writes=[r_x[tl]])
            for tl in range(ntl):
                rs, r_rs = rmsnorm_stats(X[:, tl, :], r_x[tl], D)
                P.op(DVE, lambda e, tl=tl, rs=rs: e.scalar_tensor_tensor(
                    out=hbuf[:], in0=X[:, tl, :], scalar=rs, in1=bc4[:, 0, :], op0=ALU.mult, op1=ALU.mult),
                    reads=[r_x[tl], r_rs, R["bc4"]], writes=[R["hbuf"]])
                transpose_to(lambda c0, n, tl=tl: HTv[:, c0:c0 + n, tl * 128:(tl + 1) * 128], r_hTt[tl], hbuf, R["hbuf"], 16)
            if prefix:
                state["x_parents"] = r_x
            return {"tiles": tiles, "prefix": prefix, "buf": buf, "r_x": r_x, "r_hTt": r_hTt}

        def do_block(ctx, after_head0=None):
            tiles, prefix, buf = ctx["tiles"], ctx["prefix"], ctx["buf"]
            r_x, r_hTt = ctx["r_x"], ctx["r_hTt"]
            HT = HM[:, buf]
            ntl = len(tiles)
            nt = ntl * 128
            has_sample = any(k == "m" and i == 8 for k, i in tiles)
            r_mT = res("mT", state["hm_par"][1]) if not prefix else None
            ap_par = state["arena_parents"]
            r_cs = res("cs", ap_par); r_rt = res("ropetmp", ap_par); r_wn = res("wn", ap_par)
            r_u = res("u", ap_par); r_hist = res("hist", ap_par)
            r_pA_b = [res(f"poolA{i}", ap_par) for i in range(2)]
            r_pT_b = [res("pooledT0", ap_par), r_rt]
            cosd, sind = (c_cosp, c_sinp) if prefix else (c_cosm, c_sinm)
            t0 = tiles[0][1] * 128
            dma(SP, cs_t[:, 0, 0:nt], cosd[:, t0:t0 + nt], writes=[r_cs])
            dma(SP, cs_t[:, 1, 0:nt], sind[:, t0:t0 + nt], writes=[r_cs])
            if not prefix:
                dma(SP, wn, rnw.partition_broadcast(128), writes=[r_wn])
            nxt = [None]
            need_u = (not prefix) or tiles[-1] == ("p", 7)
            if need_u:
                for j in range(2):
                    s, r_w = load_w([("w_in", 0, D, 4096 + 512 * j, 512, 0)])
                    if not prefix:
                        P.op(DVE, lambda e, j=j: e.tensor_copy(out=u_t[:, 0, :], in_=uprev[:, j * 512:(j + 1) * 512]),
                             reads=[R["uprev"]], writes=[r_u])
                    for tl in range(ntl):
                        if prefix and tl != ntl - 1:
                            continue
                        pa, pr = nextps()

                        def mmu(e, tl=tl, s=s, pa=pa):
                            ins = None
                            for k in range(16):
                                ins = e.matmul(pa, lhsT=HT[:, k, tl * 128:(tl + 1) * 128], rhs=WB[:, s, k, :],
                                               start=(k == 0), stop=(k == 15))
                            return ins
                        P.op(PE, mmu, reads=[r_hTt[tl], r_w], writes=[pr])
                        if prefix:
                            P.op(ACT, lambda e, pa=pa, j=j: e.activation(out=uprev[:, j * 512:(j + 1) * 512], in_=pa, func=AF.Copy),
                                 reads=[pr], writes=[R["uprev"]])
                        else:
                            P.op(ACT, lambda e, pa=pa, tl=tl: e.activation(out=u_t[:, tl + 1, :], in_=pa, func=AF.Copy),
                                 reads=[pr], writes=[r_u])
                    if prefix:
                        continue
                    for tl, (kind, ti) in enumerate(tiles):
                        if ti == 8:
                            dma(SP, hist_t[0:120, 0, :], spool[0:8, :, j * 512:(j + 1) * 512].rearrange("b r c -> (b r) c"), writes=[r_hist])
                            dma(SP, hist_t[0:120, 1, :], spool[8:16, :, j * 512:(j + 1) * 512].rearrange("b r c -> (b r) c"), writes=[r_hist])
                    for gg in range(2):
                        g = 2 * j + gg
                        poolA_t, r_pA = poolA_b[g % 2], r_pA_b[g % 2]
                        pooledT, r_pT = pooledT_b[g % 2], r_pT_b[g % 2]
                        dma(SP, poolA_t, c_poolA[:, g, :, :], writes=[r_pA])
                        for cc in range(2):
                            c0 = gg * 256 + cc * 128
                            pa, pr = nextps()

                            def mmp(e, pa=pa, c0=c0, poolA_t=poolA_t):
                                ins = None
                                for tl, (kind, ti) in enumerate(tiles):
                                    o = pa[:, tl * 128:(tl + 1) * 128]
                                    if ti == 8:
                                        e.matmul(o, lhsT=hist_t[0:120, 0, c0:c0 + 128], rhs=poolA_t[0:120, 3, :], start=True, stop=False)
                                        e.matmul(o, lhsT=hist_t[0:120, 1, c0:c0 + 128], rhs=poolA_t[0:120, 4, :], start=False, stop=False)
                                        ins = e.matmul(o, lhsT=u_t[:, tl + 1, c0:c0 + 128], rhs=poolA_t[:, 5, :], start=False, stop=True)
                                    else:
                                        e.matmul(o, lhsT=u_t[:, tl, c0:c0 + 128], rhs=poolA_t[:, 2, :], start=True, stop=False)
                                        ins = e.matmul(o, lhsT=u_t[:, tl + 1, c0:c0 + 128], rhs=poolA_t[:, 0 if ti == 0 else 1, :],
                                                       start=False, stop=True)
                                return ins
                            P.op(PE, mmp, reads=[r_u, r_hist, r_pA], writes=[pr])
                            P.op(ACT, lambda e, pa=pa, cc=cc, pooledT=pooledT: e.activation(out=pooledT[:, cc, 0:nt], in_=pa[:, 0:nt], func=AF.Copy),
                                 reads=[pr], writes=[r_pT])
                        for dc in range(2):
                            pa, pr = nextps()

                            def mmo(e, pa=pa, g=g, dc=dc, pooledT=pooledT):
                                e.matmul(pa[:, 0:nt], lhsT=wpool[:, g, 0, dc * 128:(dc + 1) * 128], rhs=pooledT[:, 0, 0:nt], start=True, stop=False)
                                return e.matmul(pa[:, 0:nt], lhsT=wpool[:, g, 1, dc * 128:(dc + 1) * 128], rhs=pooledT[:, 1, 0:nt], start=False, stop=True)
                            P.op(PE, mmo, reads=[r_pT, R["wpool"]], writes=[pr])
                            P.op(ACT, lambda e, pa=pa, g=g, dc=dc: e.activation(
                                out=MT[:, 8 + g * 2 + dc, 0:nt], in_=pa[:, 0:nt], func=AF.Copy, scale=psc[:, g * 2 + dc:g * 2 + dc + 1]),
                                reads=[pr, R["psc"]], writes=[r_mT])
                    for tl, (kind, ti) in enumerate(tiles):
                        if ti == 7:
                            dma(SP, npp[:, j * 512:(j + 1) * 512], u_t[113:128, tl + 1, :], reads=[r_u])
                        if ti == 8:
                            for b in range(16):
                                dma(SP, nps[b, 7:15, j * 512:(j + 1) * 512], u_t[b * 8:(b + 1) * 8, tl + 1, :], reads=[r_u])
                            for q in range(2):
                                for bl in range(8):
                                    dma(SP, nps[8 * q + bl, 0:7, j * 512:(j + 1) * 512], hist_t[bl * 15 + 8:bl * 15 + 15, q, :], reads=[r_hist])
                    last_tl = max(tl for tl, (kind, ti) in enumerate(tiles) if ti != 8)
                    P.op(DVE, lambda e, j=j, last_tl=last_tl: e.tensor_copy(out=uprev[:, j * 512:(j + 1) * 512], in_=u_t[:, last_tl + 1, :]),
                         reads=[r_u], writes=[R["uprev"]])

            samp_par = [r_u, r_hist] + r_pA_b + r_pT_b
            r_bm = res("bmask", samp_par)
            r_sb = [res(f"sb{i}", r_x) for i in range(NSB)] if has_sample else []
            r_sbb = [res(f"sbb{i}", r_x) for i in range(NSB)] if has_sample else []
            samp = {"next": 0}
            PF = 5

            def samp_load(n):
                hh, bb = divmod(n, 16)
                i = n % NSB
                dma(SP, Sb_t[i].rearrange("p (c e) -> p c e", c=2), sret[bb, hh].rearrange("(c p) e -> p c e", p=128), writes=[r_sb[i]])
            if has_sample:
                dma(SP, bmask, c_bmask[:, :, :], writes=[r_bm])
            r_qp = res("qpad", samp_par); r_km = res("kmask", samp_par)
            sbi = [0]
            deferred = []
            for h in range(4):
                rS = R[f"S{h}"]
                def head_weights(hh):
                    srcs = [("w_in", 0, D, 1024 + hh * 256, 256, 256)]
                    if not prefix:
                        srcs = [("w_in", 0, D, hh * 256, 256, 0)] + srcs
                    a = load_w(srcs)
                    srcs = [("w_in", 0, D, 2048 + hh * 256, 256, 0)]
                    if not prefix:
                        srcs.append(("w_in", 0, D, 3072 + hh * 256, 256, 256))
                    return a + load_w(srcs)
                if h == 0:
                    hw_next = head_weights(0)
                sA, r_wA, sB, r_wB = hw_next
                for qk in ([1] if prefix else [0, 1]):
                    dstT, r_dst = (qT, R["qT"]) if qk == 0 else (kT, R["kT"])
                    pas = []
                    for half in range(2):
                        pa, pr = nextps()
                        pas.append((pa, pr))

                        def mmq(e, pa=pa, qk=qk, half=half, sA=sA):
                            ins = None
                            c0 = qk * 256 + half * 128
                            for k in range(16):
                                ins = e.matmul(pa[:, 0:nt], lhsT=WB[:, sA, k, c0:c0 + 128], rhs=HT[:, k, 0:nt],
                                               start=(k == 0), stop=(k == 15))
                            return ins
                        P.op(PE, mmq, reads=r_hTt + [r_wA], writes=[pr])
                    (p1, r1), (p2, r2) = pas
                    cosv, sinv = cs_t[:, 0, 0:nt], cs_t[:, 1, 0:nt]
                    ta, tb, tc, td = (ropetmp[:, i, 0:nt] for i in range(4))
                    P.op(DVE, lambda e, p1=p1, ta=ta, cosv=cosv: e.tensor_tensor(out=ta, in0=p1[:, 0:nt], in1=cosv, op=ALU.mult), reads=[r1, r_cs], writes=[r_rt])
                    P.op(DVE, lambda e, p2=p2, tb=tb, sinv=sinv: e.tensor_tensor(out=tb, in0=p2[:, 0:nt], in1=sinv, op=ALU.mult), reads=[r2, r_cs], writes=[r_rt])
                    P.op(DVE, lambda e, p2=p2, tc=tc, cosv=cosv: e.tensor_tensor(out=tc, in0=p2[:, 0:nt], in1=cosv, op=ALU.mult), reads=[r2, r_cs], writes=[r_rt])
                    P.op(DVE, lambda e, p1=p1, td=td, sinv=sinv: e.tensor_tensor(out=td, in0=p1[:, 0:nt], in1=sinv, op=ALU.mult), reads=[r1, r_cs], writes=[r_rt])
                    P.op(DVE, lambda e, dstT=dstT, ta=ta, tb=tb: e.tensor_tensor(out=dstT[:, 0, 0:nt], in0=ta, in1=tb, op=ALU.subtract), reads=[r_rt], writes=[r_dst])
                    P.op(DVE, lambda e, dstT=dstT, tc=tc, td=td: e.tensor_tensor(out=dstT[:, 1, 0:nt], in0=tc, in1=td, op=ALU.add), reads=[r_rt], writes=[r_dst])
                for tl, (kind, ti) in enumerate(tiles):
                    smp = (ti == 8 and kind == "m")
                    pa, pr = nextps()
                    ncol = 256 if prefix else 512

                    def mmv(e, pa=pa, tl=tl, ncol=ncol, sB=sB):
                        ins = None
                        for k in range(16):
                            ins = e.matmul(pa[:, 0:ncol], lhsT=HT[:, k, tl * 128:(tl + 1) * 128], rhs=WB[:, sB, k, 0:ncol],
                                           start=(k == 0), stop=(k == 15))
                        return ins
                    P.op(PE, mmv, reads=[r_hTt[tl], r_wB], writes=[pr])
                    kd = dec[:, (12 if smp else 8) + h:(12 if smp else 8) + h + 1]
                    P.op(DVE, lambda e, pa=pa, tl=tl, kd=kd: e.tensor_scalar(out=vd[:, tl, :], in0=pa[:, 0:256], scalar1=kd, scalar2=None, op0=ALU.mult),
                         reads=[pr, R["dec"]], writes=[R["vd"]])
                    if not prefix:
                        P.op(ACT, lambda e, pa=pa, tl=tl: e.activation(out=vt[:, tl, :], in_=pa[:, 0:256], func=AF.Copy), reads=[pr], writes=[R["vt"]])
                        P.op(ACT, lambda e, pa=pa, tl=tl: e.activation(out=Gp[:, tl, :], in_=pa[:, 256:512], func=AF.Silu), reads=[pr], writes=[R["Gp"]])
                        P.op(DVE, lambda e, tl=tl, h=h: e.tensor_tensor(out=Gp[:, tl, :], in0=Gp[:, tl, :], in1=wn[:, h * 256:(h + 1) * 256], op=ALU.mult),
                             reads=[R["Gp"], r_wn], writes=[R["Gp"]])
                for tl in range(ntl):
                    pa, pr = nextps()
                    pv = pa.bitcast(BF16)

                    def trk(e, tl=tl, pv=pv):
                        e.transpose(out=pv[:, 0:128], in_=kT[:, 0, tl * 128:(tl + 1) * 128], identity=ident[:])
                        return e.transpose(out=pv[:, 128:256], in_=kT[:, 1, tl * 128:(tl + 1) * 128], identity=ident[:])
                    P.op(PE, trk, reads=[R["kT"], R["ident"]], writes=[pr])
                    P.op(ACT, lambda e, tl=tl, pv=pv: e.activation(out=ktok[:, tl, :], in_=pv[:, 0:256], func=AF.Copy), reads=[pr], writes=[R["ktok"]])
                if h < 3:
                    hw_next = head_weights(h + 1)
                if h == 0 and after_head0 is not None:
                    nxt[0] = after_head0()
                for fn in deferred:
                    fn()
                deferred = []
                if not prefix:
                    for tl, (kind, ti) in enumerate(tiles):
                        smp = (ti == 8 and kind == "m")
                        tsl = slice(tl * 128, (tl + 1) * 128)
                        pS, rpS = nextps()

                        def mms(e, pS=pS, tsl=tsl):
                            e.matmul(pS[:, 0:128], lhsT=kT[:, 0, tsl], rhs=qT[:, 0, tsl], start=True, stop=False)
                            return e.matmul(pS[:, 0:128], lhsT=kT[:, 1, tsl], rhs=qT[:, 1, tsl], start=False, stop=True)
                        P.op(PE, mms, reads=[R["kT"], R["qT"]], writes=[rpS])
                        mk = mask[:, 1 if smp else 0, h, :]
                        P.op(DVE, lambda e, pS=pS, mk=mk, tl=tl: e.tensor_tensor(out=sTm[:, tl, :], in0=pS[:, 0:128], in1=mk, op=ALU.mult),
                             reads=[rpS, R["mask"]], writes=[R[f"sTm{tl}"]])
                for tl, (kind, ti) in enumerate(tiles):
                    smp = (ti == 8 and kind == "m")
                    if smp:
                        continue
                    pP, rpP = nextps()

                    def mmP(e, pP=pP, tl=tl):
                        e.matmul(pP[:, 0:256], lhsT=ktok[:, tl, 0:128], rhs=vd[:, tl, :], start=True, stop=True)
                        return e.matmul(pP[:, 256:512], lhsT=ktok[:, tl, 128:256], rhs=vd[:, tl, :], start=True, stop=True)
                    P.op(PE, mmP, reads=[R["ktok"], R["vd"]], writes=[rpP])
                    if not prefix:
                        P.op(ACT, lambda e, h=h, tl=tl: e.activation(out=Sbf[:, tl], in_=S[:, h], func=AF.Copy), reads=[rS], writes=[R[f"Sbf{tl}"]])
                    P.op(DVE, lambda e, pP=pP, h=h: e.scalar_tensor_tensor(
                        out=S[:, h].rearrange("p c e -> p (c e)"), in0=S[:, h].rearrange("p c e -> p (c e)"),
                        scalar=float(gam[h] ** 128), in1=pP, op0=ALU.mult, op1=ALU.add),
                        reads=[rpP, rS], writes=[rS])
                    if (kind, ti) == ("m", 7):
                        dma(SP, nrp[h].rearrange("(c p) e -> p c e", p=128), S[:, h], reads=[rS])
                if prefix:
                    continue
                for tl, (kind, ti) in enumerate(tiles):
                    smp = (ti == 8 and kind == "m")
                    tsl = slice(tl * 128, (tl + 1) * 128)
                    i2 = tl % 2
                    pO, rpO = (psum[:, 7, :], R["ps7"]) if smp else nextps()
                    if not smp:
                        def mmoc(e, pO=pO, tsl=tsl, tl=tl):
                            e.matmul(pO[:, 256:512], lhsT=qT[:, 0, tsl], rhs=Sbf[:, tl, 0, :], start=True, stop=False)
                            return e.matmul(pO[:, 256:512], lhsT=qT[:, 1, tsl], rhs=Sbf[:, tl, 1, :], start=False, stop=True)
                        P.op(PE, mmoc, reads=[R["qT"], R[f"Sbf{tl}"]], writes=[rpO])
                    else:
                        for half in range(2):
                            P.op(DVE, lambda e, tl=tl, half=half: e.tensor_tensor(
                                out=kmask_t[:, half], in0=ktok[:, tl, half * 128:(half + 1) * 128].unsqueeze(1).to_broadcast([128, 16, 128]),
                                in1=rowmask[:].unsqueeze(2).to_broadcast([128, 16, 128]), op=ALU.mult),
                                reads=[R["ktok"], R["rowmask"]], writes=[r_km])
                            P.op(DVE, lambda e, tsl=tsl, half=half: e.tensor_tensor(
                                out=qpad_t[:, half], in0=qT[:, half, tsl].unsqueeze(1).to_broadcast([128, 16, 128]),
                                in1=bmask, op=ALU.mult), reads=[R["qT"], r_bm], writes=[r_qp])
                        for b in range(16):
                            n = h * 16 + b
                            while samp["next"] < min(64, n + PF + 1):
                                samp_load(samp["next"])
                                samp["next"] += 1
                            i = n % NSB
                            P.op(ACT, lambda e, i=i: e.activation(out=Sbb_t[i], in_=Sb_t[i], func=AF.Copy),
                                 reads=[r_sb[i]], writes=[r_sbb[i]])

                            def mmsc(e, pO=pO, b=b, i=i):
                                e.matmul(pO[:, 256:512], lhsT=qpad_t[:, 0, b, :], rhs=Sbb_t[i][:, 0:256], start=(b == 0), stop=False)
                                return e.matmul(pO[:, 256:512], lhsT=qpad_t[:, 1, b, :], rhs=Sbb_t[i][:, 256:512], start=False, stop=(b == 15))
                            P.op(PE, mmsc, reads=[r_qp, r_sbb[i]], writes=[rpO])
                            pU, rpU = nextps()

                            def mmsu(e, pU=pU, b=b, tl=tl):
                                e.matmul(pU[:, 0:256], lhsT=kmask_t[:, 0, b, :], rhs=vd[:, tl, :], start=True, stop=True)
                                return e.matmul(pU[:, 256:512], lhsT=kmask_t[:, 1, b, :], rhs=vd[:, tl, :], start=True, stop=True)
                            P.op(PE, mmsu, reads=[r_km, R["vd"]], writes=[rpU])
                            P.op(DVE, lambda e, pU=pU, i=i, h=h: e.scalar_tensor_tensor(
                                out=Sb_t[i], in0=Sb_t[i], scalar=float(gam[h] ** 8), in1=pU, op0=ALU.mult, op1=ALU.add),
                                reads=[rpU, r_sb[i]], writes=[r_sb[i]])
                            dma(POOL, nrs[b, h].rearrange("(c p) e -> p c e", p=128), Sb_t[i].rearrange("p (c e) -> p c e", c=2), reads=[r_sb[i]])

                    def mmoi(e, pO=pO, tl=tl):
                        return e.matmul(pO[:, 0:256], lhsT=sTm[:, tl, :], rhs=vt[:, tl, :], start=True, stop=True)
                    P.op(PE, mmoi, reads=[R[f"sTm{tl}"], R["vt"]], writes=[rpO])
                    P.op(ACT, lambda e, pO=pO, i2=i2: e.activation(out=oA[:, i2, :], in_=pO[:, 0:256], func=AF.Copy), reads=[rpO], writes=[R[f"oA{i2}"]])
                    qd = dec[:, (4 if smp else 0) + h:(4 if smp else 0) + h + 1]
                    P.op(DVE, lambda e, pO=pO, i2=i2, qd=qd: e.scalar_tensor_tensor(
                        out=oB[:, i2, :], in0=pO[:, 256:512], scalar=qd, in1=oA[:, i2, :], op0=ALU.mult, op1=ALU.add),
                        reads=[rpO, R[f"oA{i2}"], R["dec"]], writes=[R[f"oB{i2}"]])
                    rs, r_rs = rmsnorm_stats(oB[:, i2, :], R[f"oB{i2}"], 256)
                    P.op(DVE, lambda e, i2=i2, rs=rs, tl=tl: e.scalar_tensor_tensor(
                        out=rr[:, tl, :], in0=oB[:, i2, :], scalar=rs, in1=Gp[:, tl, :], op0=ALU.mult, op1=ALU.mult),
                        reads=[R[f"oB{i2}"], r_rs, R["Gp"]], writes=[R[f"rr{tl}"]])

                    def stage4(tl=tl, h=h, tsl=tsl):
                        pa, pr = nextps()
                        pv = pa.bitcast(BF16)

                        def trr(e, pv=pv, tl=tl):
                            e.transpose(out=pv[:, 0:128], in_=rr[:, tl, 0:128], identity=ident[:])
                            return e.transpose(out=pv[:, 128:256], in_=rr[:, tl, 128:256], identity=ident[:])
                        P.op(PE, trr, reads=[R[f"rr{tl}"], R["ident"]], writes=[pr])
                        P.op(ACT, lambda e, pv=pv, h=h, tsl=tsl: e.activation(
                            out=MT[:, 2 * h:2 * h + 2, tsl], in_=pv[:, 0:256].rearrange("p (c t) -> p c t", c=2), func=AF.Copy),
                            reads=[pr], writes=[r_mT])
                    deferred.append(stage4)
            for fn in deferred:
                fn()
            deferred = []

            ph1 = [r_cs, r_rt, r_wn, r_u, r_hist, r_qp, r_km, r_bm] + r_pA_b + r_pT_b
            if prefix:
                state["arena_parents"] = ph1
                state["hm_par"][buf] = list(r_hTt)
                return nxt[0]

            r_mo = [res(f"mixout{t}", ph1) for t in range(ntl)]
            for cb in range(4):
                s, r_w = load_w([("w_out", 0, D, cb * 512, 512, 0)])
                for tl in range(ntl):
                    pa, pr = nextps()

                    def mmw(e, pa=pa, tl=tl, s=s):
                        ins = None
                        for k in range(16):
                            ins = e.matmul(pa, lhsT=MT[:, k, tl * 128:(tl + 1) * 128], rhs=WB[:, s, k, :], start=(k == 0), stop=(k == 15))
                        return ins
                    P.op(PE, mmw, reads=[r_mT, r_w], writes=[pr])
                    P.op(ACT, lambda e, pa=pa, tl=tl, cb=cb: e.activation(out=mixout[:, tl, cb * 512:(cb + 1) * 512], in_=pa, func=AF.Copy),
                         reads=[pr], writes=[r_mo[tl]])
            r_hfT = res("hfT", r_hTt)
            if has_sample:
                for tl, (kind, ti) in enumerate(tiles):
                    r_x[tl] = res(f"x{tl}", r_sb + r_sbb)
                    dma(SP, X[:, tl, :], xm[ti * 128:(ti + 1) * 128, :], writes=[r_x[tl]])
            for tl in range(ntl):
                rs, r_rs = rmsnorm_stats(mixout[:, tl, :], r_mo[tl], D)
                P.op(DVE, lambda e, tl=tl, rs=rs: e.scalar_tensor_tensor(
                    out=mixout[:, tl, :], in0=mixout[:, tl, :], scalar=rs, in1=bc4[:, 1, :], op0=ALU.mult, op1=ALU.mult),
                    reads=[r_mo[tl], r_rs, R["bc4"]], writes=[r_mo[tl]])
                P.op(DVE, lambda e, tl=tl: e.tensor_tensor(out=X[:, tl, :], in0=X[:, tl, :], in1=mixout[:, tl, :], op=ALU.add),
                     reads=[r_mo[tl], r_x[tl]], writes=[r_x[tl]])
                rs2, r_rs2 = rmsnorm_stats(X[:, tl, :], r_x[tl], D)
                P.op(DVE, lambda e, tl=tl, rs2=rs2: e.scalar_tensor_tensor(
                    out=hbuf[:], in0=X[:, tl, :], scalar=rs2, in1=bc4[:, 2, :], op0=ALU.mult, op1=ALU.mult),
                    reads=[r_x[tl], r_rs2, R["bc4"]], writes=[R["hbuf"]])
                transpose_to(lambda c0, n, tl=tl: HT[:, c0:c0 + n, tl * 128:(tl + 1) * 128], r_hfT, hbuf, R["hbuf"], 16)

            r_act = [res(f"act{f}", r_mo) for f in range(NF)]
            sgv = [oA[:].rearrange("p a b -> p (a b)")[:, 0:nt], oB[:].rearrange("p a b -> p (a b)")[:, 0:nt],
                   rr[:].rearrange("p a b -> p (a b)").bitcast(F32)[:, 0:nt],
                   Sbf[:].rearrange("p a c e -> p (a c e)").bitcast(F32)[:, 0:nt]]
            sgr = [[R["oA0"], R["oA1"]], [R["oB0"], R["oB1"]], [R["rr0"], R["rr1"], R["rr2"]], [R["Sbf0"], R["Sbf1"]]]
            for fg in range(NF // 4):
                sg_, r_wg = load_w([("w_gate", 0, D, fg * 512, 512, 0)])
                su_, r_wu = load_w([("w_up", 0, D, fg * 512, 512, 0)])
                for fc in range(4):
                    pg, rpg = nextps()

                    def mmg(e, pg=pg, fc=fc, s=sg_):
                        ins = None
                        for k in range(16):
                            ins = e.matmul(pg[:, 0:nt], lhsT=WB[:, s, k, fc * 128:(fc + 1) * 128], rhs=HT[:, k, 0:nt], start=(k == 0), stop=(k == 15))
                        return ins
                    P.op(PE, mmg, reads=[r_hfT, r_wg], writes=[rpg])
                    P.op(ACT, lambda e, pg=pg, fc=fc: e.activation(out=sgv[fc], in_=pg[:, 0:nt], func=AF.Silu), reads=[rpg], writes=sgr[fc])
                for fc in range(4):
                    f = fg * 4 + fc
                    pu, rpu = nextps()

                    def mmu2(e, pu=pu, fc=fc, s=su_):
                        ins = None
                        for k in range(16):
                            ins = e.matmul(pu[:, 0:nt], lhsT=WB[:, s, k, fc * 128:(fc + 1) * 128], rhs=HT[:, k, 0:nt], start=(k == 0), stop=(k == 15))
                        return ins
                    P.op(PE, mmu2, reads=[r_hfT, r_wu], writes=[rpu])
                    P.op(DVE, lambda e, pu=pu, fc=fc, f=f: e.tensor_tensor(out=actT[:, f, 0:nt], in0=pu[:, 0:nt], in1=sgv[fc], op=ALU.mult),
                         reads=[rpu] + sgr[fc], writes=[r_act[f]])
            r_ff = [res(f"ff{t}", [r_hfT, r_mT] + r_hTt) for t in range(ntl)]
            for cb in range(4):
                pbs = [nextps() for _ in range(ntl)]
                for (k0, nk) in ((0, 16), (16, 16), (32, 12)):
                    s, r_w = load_w([("w_down", k0 * 128, nk * 128, cb * 512, 512, 0)])

                    def mmd(e, k0=k0, nk=nk, s=s, pbs=pbs):
                        ins = None
                        for kk in range(nk):
                            for tl in range(ntl):
                                ins = e.matmul(pbs[tl][0], lhsT=actT[:, k0 + kk, tl * 128:(tl + 1) * 128], rhs=WB[:, s, kk, :],
                                               start=(k0 + kk == 0), stop=(k0 + kk == NF - 1))
                        return ins
                    P.op(PE, mmd, reads=r_act[k0:k0 + nk] + [r_w], writes=[pb[1] for pb in pbs])
                for tl in range(ntl):
                    P.op(ACT, lambda e, tl=tl, cb=cb, pa=pbs[tl][0]: e.activation(out=FFO[:, tl, cb * 512:(cb + 1) * 512], in_=pa, func=AF.Copy),
                         reads=[pbs[tl][1]], writes=[r_ff[tl]])
            for tl, (kind, ti) in enumerate(tiles):
                rs, r_rs = rmsnorm_stats(FFO[:, tl, :], r_ff[tl], D)
                P.op(DVE, lambda e, tl=tl, rs=rs: e.scalar_tensor_tensor(
                    out=FFO[:, tl, :], in0=FFO[:, tl, :], scalar=rs, in1=bc4[:, 3, :], op0=ALU.mult, op1=ALU.mult),
                    reads=[r_ff[tl], r_rs, R["bc4"]], writes=[r_ff[tl]])
                P.op(DVE, lambda e, tl=tl: e.tensor_tensor(out=X[:, tl, :], in0=X[:, tl, :], in1=FFO[:, tl, :], op=ALU.add),
                     reads=[r_ff[tl], r_x[tl]], writes=[r_x[tl]])
                dma(SP, ym[ti * 128:(ti + 1) * 128, :], X[:, tl, :], reads=[r_x[tl]])
            state["arena_parents"] = r_act
            state["hm_par"] = [list(r_ff), list(r_ff)]
            state["x_parents"] = r_x
            return nxt[0]


        plan = [([("p", i) for i in blk], True, b) for blk, b in (([0, 1, 2], 1), ([3, 4, 5], 0), ([6, 7], 1))]
        plan += [([("m", i) for i in blk], False, 0) for blk in ([0, 1, 2], [3, 4, 5], [6, 7, 8])]
        ctx = phase0(*plan[0])
        for i in range(len(plan)):
            nxt_plan = plan[i + 1] if i + 1 < len(plan) else None
            if plan[i][1] and nxt_plan is not None:
                ctx_next = do_block(ctx, after_head0=lambda p=nxt_plan: phase0(*p))
            else:
                do_block(ctx)
                ctx_next = phase0(*nxt_plan) if nxt_plan is not None else None
            ctx = ctx_next
        P.emit()
        build_nc.stats = P.stats
    return nc


def _consts(half):
    gam = np.array([1.0 - 2.0 ** (-5.0 - h) for h in range(4)], np.float64)
    c = {}
    c["c_ident"] = np.eye(128, dtype=np.float32).astype(ml_dtypes.bfloat16)
    inv_freq = 1.0 / (10000.0 ** (np.arange(0, 256, 2, dtype=np.float64) / 256.0))

    def tab(pos):
        a64 = pos.astype(np.float64)[None, :] * inv_freq[:, None]
        return np.cos(a64).astype(np.float32), np.sin(a64).astype(np.float32)
    posm = np.concatenate([half * 1024 + np.arange(1024), 16384 + (np.arange(128) % 8)]).astype(np.float32)
    c["c_cosm"], c["c_sinm"] = tab(posm)
    c["c_cosp"], c["c_sinp"] = tab(np.arange(1024).astype(np.float32))
    j = np.arange(128)[:, None]; i = np.arange(128)[None, :]
    mask = np.zeros((128, 2, 4, 128), np.float64)
    for h in range(4):
        mp = np.where(i >= j, gam[h] ** np.maximum(i - j, 0), 0.0) / 16.0
        ms = np.where((i >= j) & (i // 8 == j // 8), gam[h] ** np.maximum(i - j, 0), 0.0) / 16.0
        mask[:, 0, h, :] = mp
        mask[:, 1, h, :] = ms
    c["c_mask"] = mask.astype(np.float32)
    dec = np.zeros((128, 16), np.float64)
    t = np.arange(128)
    for h in range(4):
        dec[:, h] = gam[h] ** (t + 1)
        dec[:, 4 + h] = gam[h] ** ((t % 8) + 1)
        dec[:, 8 + h] = gam[h] ** (127 - t) / 16.0
        dec[:, 12 + h] = gam[h] ** (7 - (t % 8)) / 16.0
    c["c_dec"] = dec.astype(np.float32)
    bm = (np.arange(128)[None, :] // 8 == np.arange(16)[:, None]).astype(np.float32)
    c["c_bmask"] = np.broadcast_to(bm[None], (128, 16, 128)).astype(ml_dtypes.bfloat16)
    c["c_rowmask"] = (np.arange(128)[:, None] // 8 == np.arange(16)[None, :]).astype(np.float32)
    A = np.zeros((128, 4, 6, 128), np.float64)
    s = np.arange(128)[:, None]; tt = np.arange(128)[None, :]
    for g, w in enumerate((2, 4, 8, 16)):
        inwin = ((s <= tt) & (s > tt - w)).astype(np.float64)
        eye = (s == tt).astype(np.float64)
        cnt_first = np.minimum(tt + 1, w).astype(np.float64)
        A[:, g, 1] = inwin / w - eye
        A[:, g, 0] = (inwin / cnt_first - eye) if half == 0 else A[:, g, 1]
        A[:, g, 2] = (s >= 128 + tt - w + 1).astype(np.float64) / w
        for q in range(2):
            M = np.zeros((128, 128))
            for bl in range(8):
                b = 8 * q + bl
                for r in range(15):
                    for t8 in range(8):
                        if r >= 16 + t8 - w:
                            M[bl * 15 + r, b * 8 + t8] = 1.0 / w
            A[:, g, 3 + q] = M
        M = np.zeros((128, 128))
        for b in range(16):
            for s8 in range(8):
                for t8 in range(8):
                    v = 0.0
                    if s8 <= t8 and s8 >= t8 - w + 1:
                        v += 1.0 / w
                    if s8 == t8:
                        v -= 1.0
                    M[b * 8 + s8, b * 8 + t8] = v
        A[:, g, 5] = M
    c["c_poolA"] = A.astype(np.float32)
    return c


_CACHE = {}


def make_in_maps(x_prompt, x_sample, state_ret, state_pool, norm_mix_pre, norm_mix_post, w_in, ret_norm_w,
                 w_pool, pool_scale, w_out, norm_ffn_pre, norm_ffn_post, w_gate, w_up, w_down, cores=None):
    f = lambda a: np.ascontiguousarray(np.asarray(a, dtype=np.float32))
    x_prompt, x_sample, state_ret, state_pool = f(x_prompt), f(x_sample), f(state_ret), f(state_pool)
    shared = {
        "w_in": f(w_in), "w_out": f(w_out), "w_gate": f(w_gate), "w_up": f(w_up), "w_down": f(w_down),
        "w_pool": f(w_pool), "nmp": f(norm_mix_pre), "nmpo": f(norm_mix_post), "nfp": f(norm_ffn_pre),
        "nfpo": f(norm_ffn_post), "rnw": f(ret_norm_w), "pscale": f(pool_scale),
    }
    if "consts" not in _CACHE:
        _CACHE["consts"] = [_consts(0), _consts(1)]
    in_maps = []
    for c in (range(NCORES) if cores is None else cores):
        b, half = c // 2, c % 2
        m = dict(shared)
        m.update(_CACHE["consts"][half])
        xs = x_sample[16 * c:16 * (c + 1)].reshape(128, D)
        m["xm"] = np.ascontiguousarray(np.concatenate([x_prompt[b, half * 1024:(half + 1) * 1024], xs], axis=0))
        m["xp"] = np.ascontiguousarray(x_prompt[b, 0:1024]) if half == 1 else np.zeros((1024, D), np.float32)
        m["sret"] = np.ascontiguousarray(state_ret[16 * c:16 * (c + 1)])
        m["spool"] = np.ascontiguousarray(state_pool[16 * c:16 * (c + 1)])
        in_maps.append(m)
    return in_maps


def kernel(**inputs):
    if "nc" not in _CACHE:
        _CACHE["nc"] = build_nc()
    nc = _CACHE["nc"]
    in_maps = make_in_maps(**inputs)
    res = run_bass_kernel_spmd(nc, in_maps, core_ids=list(range(NCORES)))
    outs = res.results
    y_prompt = np.empty((4, 2048, D), np.float32)
    y_sample = np.empty((128, 8, D), np.float32)
    nrp = np.empty((4, 4, 256, 256), np.float32)
    npp = np.empty((4, 15, 1024), np.float32)
    nrs = np.empty((128, 4, 256, 256), np.float32)
    nps = np.empty((128, 15, 1024), np.float32)
    for c in range(NCORES):
        b, half = c // 2, c % 2
        o = outs[c]
        y_prompt[b, half * 1024:(half + 1) * 1024] = o["ym"][0:1024]
        y_sample[16 * c:16 * (c + 1)] = o["ym"][1024:1152].reshape(16, 8, D)
        nrs[16 * c:16 * (c + 1)] = o["nrs"]
        nps[16 * c:16 * (c + 1)] = o["nps"]
        if half == 1:
            nrp[b] = o["nrp"]
            npp[b] = o["npp"]
    return (y_prompt, y_sample, nrp, npp, nrs, nps)
```
